# Optimizing a Trainium2 kernel written in Bass

```python
import jax, jax.numpy as jnp
from jax import lax
import numpy as np

D_MODEL = 1024
BATCH = 8
SEQ = 2048
DEPTH = 1
DEC_BATCH = 128
DEC_SEQ = 8
PAST_LEN = 16384
PAGE_SIZE = 128

MIX_W = D_MODEL
GROUP_W = MIX_W // 2
N_HEADS_A = 8
N_HEADS_B = 8
HEAD_DIM = GROUP_W // N_HEADS_A
K_A = 3
K_B = 31
IN_COLS = 5 * GROUP_W
D_FF = ((8 * D_MODEL // 3 + 255) // 256) * 256
EPS = 1e-6

kernel_name = "hybrid_shortconv_conformer_decode_step"


def rmsnorm(x, g):
    xf = x.astype(jnp.float32)
    r = xf * lax.rsqrt(jnp.mean(xf * xf, axis=-1, keepdims=True) + EPS)
    return (r * g.astype(jnp.float32)).astype(x.dtype)


def layernorm(x, g, b):
    xf = x.astype(jnp.float32)
    mu = jnp.mean(xf, axis=-1, keepdims=True)
    xc = xf - mu
    var = jnp.mean(xc * xc, axis=-1, keepdims=True)
    r = xc * lax.rsqrt(var + EPS) * g.astype(jnp.float32) + b.astype(jnp.float32)
    return r.astype(x.dtype)


def causal_dwconv(u, hist, w):
    k, c = w.shape
    up = jnp.concatenate([hist.astype(u.dtype), u], axis=1)
    y = lax.conv_general_dilated(up, w.astype(u.dtype)[:, None, :], window_strides=(1,),
                                 padding='VALID', dimension_numbers=('NWC', 'WIO', 'NWC'),
                                 feature_group_count=c)
    return y, up[:, up.shape[1] - (k - 1):]


def layer(x, hist_a, hist_b, g_mix, w_in, conv_a_w, conv_b_w, conv_b_bias, ln_b_g, ln_b_b,
          w_out, g_ffn, w_gate, w_up, w_down):
    h = rmsnorm(x, g_mix)
    z = jnp.einsum('btd,dc->btc', h, w_in)
    b_gate, c_gate, v, glu_val, glu_gate = jnp.split(z, 5, axis=-1)
    conv_a, new_a = causal_dwconv(c_gate * v, hist_a, conv_a_w)
    y_a = b_gate * conv_a
    u_b = glu_val * jax.nn.sigmoid(glu_gate)
    conv_b, new_b = causal_dwconv(u_b, hist_b, conv_b_w)
    y_b = jax.nn.silu(layernorm(conv_b + conv_b_bias.astype(conv_b.dtype), ln_b_g, ln_b_b))
    x = x + jnp.einsum('btc,cd->btd', jnp.concatenate([y_a, y_b], axis=-1), w_out)
    h2 = rmsnorm(x, g_ffn)
    f = jax.nn.silu(jnp.einsum('btd,df->btf', h2, w_gate)) * jnp.einsum('btd,df->btf', h2, w_up)
    x = x + jnp.einsum('btf,fd->btd', f, w_down)
    return x, new_a, new_b


def setup_inputs(seed: int = 0) -> dict:
    key = jax.random.key(seed)
    ks = jax.random.split(key, 20)
    f32 = jnp.float32
    nrm = lambda k, s, sc: (jax.random.normal(k, s, f32) * sc).astype(f32)
    return {
        "x_prompt": nrm(ks[0], (BATCH, SEQ, D_MODEL), 1.0),
        "x_sample": nrm(ks[1], (DEC_BATCH, DEC_SEQ, D_MODEL), 1.0),
        "state_conv_a": nrm(ks[2], (DEPTH, DEC_BATCH, K_A - 1, GROUP_W), 1.0),
        "state_conv_b": nrm(ks[3], (DEPTH, DEC_BATCH, K_B - 1, GROUP_W), 1.0),
        "g_mix": 1.0 + nrm(ks[4], (DEPTH, D_MODEL), 0.1),
        "w_in": nrm(ks[5], (DEPTH, D_MODEL, IN_COLS), D_MODEL ** -0.5),
        "conv_a_w": nrm(ks[6], (DEPTH, K_A, GROUP_W), K_A ** -0.5),
        "conv_b_w": nrm(ks[7], (DEPTH, K_B, GROUP_W), K_B ** -0.5),
        "conv_b_bias": nrm(ks[8], (DEPTH, GROUP_W), 0.02),
        "ln_b_g": 1.0 + nrm(ks[9], (DEPTH, GROUP_W), 0.1),
        "ln_b_b": nrm(ks[10], (DEPTH, GROUP_W), 0.02),
        "w_out": nrm(ks[11], (DEPTH, MIX_W, D_MODEL), MIX_W ** -0.5),
        "g_ffn": 1.0 + nrm(ks[12], (DEPTH, D_MODEL), 0.1),
        "w_gate": nrm(ks[13], (DEPTH, D_MODEL, D_FF), D_MODEL ** -0.5),
        "w_up": nrm(ks[14], (DEPTH, D_MODEL, D_FF), D_MODEL ** -0.5),
        "w_down": nrm(ks[15], (DEPTH, D_FF, D_MODEL), D_FF ** -0.5),
        "g_final": 1.0 + nrm(ks[16], (D_MODEL,), 0.1),
    }


def reference(x_prompt, x_sample, state_conv_a, state_conv_b, g_mix, w_in, conv_a_w, conv_b_w,
              conv_b_bias, ln_b_g, ln_b_b, w_out, g_ffn, w_gate, w_up, w_down, g_final):
    xp, xs = x_prompt, x_sample
    bp = x_prompt.shape[0]
    pa, pb, sa, sb = [], [], [], []
    for l in range(DEPTH):
        params = (g_mix[l], w_in[l], conv_a_w[l], conv_b_w[l], conv_b_bias[l], ln_b_g[l], ln_b_b[l],
                  w_out[l], g_ffn[l], w_gate[l], w_up[l], w_down[l])
        za = jnp.zeros((bp, K_A - 1, GROUP_W), xp.dtype)
        zb = jnp.zeros((bp, K_B - 1, GROUP_W), xp.dtype)
        xp, npa, npb = layer(xp, za, zb, *params)
        xs, nsa, nsb = layer(xs, state_conv_a[l], state_conv_b[l], *params)
        pa.append(npa); pb.append(npb); sa.append(nsa); sb.append(nsb)
    y_prompt = rmsnorm(xp, g_final)
    y_sample = rmsnorm(xs, g_final)
    new_conv_a_prompt = jnp.stack(pa, axis=0)
    new_conv_b_prompt = jnp.stack(pb, axis=0)
    new_conv_a_sample = jnp.stack(sa, axis=0)
    new_conv_b_sample = jnp.stack(sb, axis=0)
    return (y_prompt, y_sample, new_conv_a_prompt, new_conv_b_prompt, new_conv_a_sample, new_conv_b_sample)
```

```python
import numpy as np
from contextlib import ExitStack
import concourse.bass as bass
import concourse.mybir as mybir
from concourse.bass_utils import run_bass_kernel_spmd

F32 = mybir.dt.float32
F32R = mybir.dt.float32r
AF = mybir.ActivationFunctionType
ALU = mybir.AluOpType

NCORES = 8
D = 1024
GW = 512
DFF = 2816
SEQ = 2048
NSEQ_S = 16
TS = 8
KA = 3
KB = 31
EPS = 1e-6
TGMAX = 768
NSLOT = 4

P_GMIX = 0
P_GFFN = 8
P_GFIN = 16
P_CAW = 24
P_CBW = 36
P_CBB = 160
P_LNG = 164
P_LNB = 168
NPARAM = 172

GROUPS = [
    (0, 768, 0, [(0, 512), (512, 768)], [(0, 512), (512, 768)]),
    (768, 768, 0, [(0, 512), (512, 768)], [(0, 512), (512, 768)]),
    (1536, 512, 128, [(0, 384), (384, 640)], [(0, 256), (256, 512)]),
]


class Buf:
    __slots__ = ("w", "r", "name")

    def __init__(self, name=""):
        self.w = None
        self.r = {}
        self.name = name


class Sched:
    ENGS = ["pe", "act", "dve", "pool", "sp"]

    def __init__(self, nc, es, n_dma_sems=12):
        self.nc = nc
        self.q = {e: [] for e in self.ENGS}
        self.sem = {e: es.enter_context(nc.semaphore("s_" + e)) for e in ["pe", "act", "dve", "pool"]}
        self.cnt = {e: 0 for e in self.sem}
        self.waited = {e: {} for e in self.ENGS}
        self.dsem = {q: [es.enter_context(nc.semaphore("d%s%d" % (q, i))) for i in range(n_dma_sems)] for q in ("sp", "pool")}
        self.dcnt = {q: [0] * n_dma_sems for q in ("sp", "pool")}
        self.dnext = {"sp": 0, "pool": 0}
        self.nwait = 0

    def _waits(self, eng, deps):
        waits = []
        for tok in deps:
            if tok is None:
                continue
            key, sh, val = tok
            if self.waited[eng].get(key, 0) >= val:
                continue
            self.waited[eng][key] = val
            waits.append((sh, val))
        self.nwait += len(waits)
        return waits

    @staticmethod
    def _deps(reads, writes):
        deps = []
        for b in reads:
            deps.append(b.w)
        for b in writes:
            deps.append(b.w)
            deps.extend(b.r.values())
        return deps

    @staticmethod
    def _commit(eng, tok, reads, writes):
        for b in reads:
            b.r[tok[0]] = tok
        for b in writes:
            b.w = tok
            b.r = {}

    def op(self, eng, fn, reads=(), writes=(), extra=()):
        waits = self._waits(eng, self._deps(reads, writes) + list(extra))
        self.cnt[eng] += 1
        tok = (eng, self.sem[eng], self.cnt[eng])

        def run(e, waits=waits, fn=fn, tok=tok):
            for sh, val in waits:
                e.wait_ge(sh, val)
            fn(e).then_inc(tok[1], 1)

        self.q[eng].append(run)
        self._commit(eng, tok, reads, writes)
        return tok

    def mm_group(self, out_ap, pairs, reads, writes):
        n = len(pairs)
        waits = self._waits("pe", self._deps(reads, writes))
        self.cnt["pe"] += 1
        tok = ("pe", self.sem["pe"], self.cnt["pe"])

        def run(e, waits=waits, pairs=pairs, tok=tok, out_ap=out_ap):
            for sh, val in waits:
                e.wait_ge(sh, val)
            ins = None
            for i, (l, r) in enumerate(pairs):
                ins = e.matmul(out_ap, l, r, start=(i == 0), stop=(i == n - 1))
            ins.then_inc(tok[1], 1)

        self.q["pe"].append(run)
        self._commit("pe", tok, reads, writes)
        return tok

    def dma(self, out_ap, in_ap, reads=(), writes=(), eng="sp"):
        i = self.dnext[eng]
        dsem, dcnt = self.dsem[eng], self.dcnt[eng]
        self.dnext[eng] = (i + 1) % len(dsem)
        key = "d%s%d" % (eng, i)
        prev = (key, dsem[i], 16 * dcnt[i]) if dcnt[i] else None
        waits = self._waits(eng, self._deps(reads, writes) + [prev])
        dcnt[i] += 1
        tok = (key, dsem[i], 16 * dcnt[i])

        def run(e, waits=waits, tok=tok):
            for sh, val in waits:
                e.wait_ge(sh, val)
            e.dma_start(out=out_ap, in_=in_ap).then_inc(tok[1], 16)

        self.q[eng].append(run)
        self._commit(eng, tok, reads, writes)
        return tok

    def finish(self, final_toks):
        nc = self.nc
        waits = self._waits("sp", final_toks)

        def fin(e, waits=waits):
            for sh, val in waits:
                e.wait_ge(sh, val)

        self.q["sp"].append(fin)
        q = self.q
        with nc.Block() as block:
            @block.tensor
            def _(e):
                for f in q["pe"]:
                    f(e)

            @block.scalar
            def _(e):
                for f in q["act"]:
                    f(e)

            @block.vector
            def _(e):
                for f in q["dve"]:
                    f(e)

            @block.gpsimd
            def _(e):
                for f in q["pool"]:
                    f(e)

            @block.sync
            def _(e):
                for f in q["sp"]:
                    f(e)


def build_nc():
    nc = bass.Bass("TRN2", target_bir_lowering=False)
    dt_in = lambda n, s: nc.dram_tensor(n, s, F32, kind="ExternalInput").ap()
    dt_out = lambda n, s: nc.dram_tensor(n, s, F32, kind="ExternalOutput").ap()
    xpT_d = dt_in("xpT", [D, SEQ])
    xsT_d = dt_in("xsT", [D, NSEQ_S * TS])
    ubs_d = dt_in("ubs_in", [128, 4 * NSEQ_S * (30 + TS)])
    cvs_d = dt_in("cvs_in", [128, 4 * NSEQ_S * (2 + TS)])
    w8_d = dt_in("w8", [36, 128, 2048])
    wd_d = dt_in("wd", [16, 128, 1408])
    par_d = dt_in("params", [128, NPARAM])
    id_d = dt_in("ident", [128, 128])
    dgw_d = dt_in("dgw", [8, 128, 2048])
    ypT_d = dt_out("ypT", [D, SEQ])
    ysT_d = dt_out("ysT", [D, NSEQ_S * TS])
    ubs_o = dt_out("ubs_out", [128, 4 * NSEQ_S * (30 + TS)])
    cvs_o = dt_out("cvs_out", [128, 4 * NSEQ_S * (2 + TS)])
    nbp_o = dt_out("nbp_out", [128, 4 * 30])
    nap_o = dt_out("nap_out", [128, 4 * 2])

    with ExitStack() as es:
        S = Sched(nc, es)
        sb = lambda n, s, d=F32: es.enter_context(nc.sbuf_tensor(n, s, d))
        X = sb("X", [128, 2, 8, TGMAX])
        hT = sb("hT", [128, 8, TGMAX], F32R)
        yfT = sb("yfT", [128, 11, TGMAX], F32R)
        cbT = yfT[:, 4:8, :]
        caT = yfT[:, 0:4, :]
        accS = sb("accS", [128, 2, 4, NSEQ_S * TS])
        wsl = sb("wsl", [128, NSLOT, 2048], F32R)
        cvp = sb("cvp", [128, 2, 2 + TGMAX])
        cvs = sb("cvs", [128, 4, NSEQ_S, 2 + TS])
        cvh = sb("cvh", [128, 4, 2])
        ubp = sb("ubp", [128, 2, 30 + TGMAX], F32R)
        ubs = sb("ubs", [128, 4, NSEQ_S, 30 + TS])
        ubh = sb("ubh", [128, 4, 30], F32R)
        sq = sb("sq", [128, 2, 512], F32R)
        tmp = sb("tmp", [128, 3, 512])
        bcr = sb("bcr", [128, TGMAX])
        bcm = sb("bcm", [128, TGMAX])
        bcf = sb("bcf", [128, TGMAX])
        ost = sb("ost", [128, 2, 512])
        par = sb("par", [128, NPARAM])
        idt = sb("idt", [128, 128])
        ones_f = sb("ones_f", [128, 128])
        ones = sb("ones", [128, 128], F32R)
        epsb = sb("epsb", [128, 1])
        dmy = sb("dmy", [128, 2])
        diag = sb("diag", [128, 2, 16, 128], F32R)
        ps = es.enter_context(nc.psum_tensor("ps", [128, 8, 512], F32))

        b_ps = [Buf("ps%d" % i) for i in range(8)]
        ps_next = [0]

        def psum():
            i = ps_next[0]
            ps_next[0] = (i + 1) % 6
            return i

        NB = 2
        b_xx = [[[Buf() for _ in range(NB)] for _ in range(8)] for _ in range(2)]
        b_h = [[Buf() for _ in range(NB)] for _ in range(8)]
        b_y = [[Buf() for _ in range(NB)] for _ in range(11)]
        b_accS = [[Buf() for _ in range(4)] for _ in range(2)]
        b_w = [Buf() for _ in range(NSLOT)]
        b_cvp = [Buf(), Buf()]
        b_cvs = [Buf() for _ in range(4)]
        b_cvh = [Buf() for _ in range(4)]
        b_ubp = [Buf(), Buf()]
        b_ubs = [Buf() for _ in range(4)]
        b_ubh = [Buf() for _ in range(4)]
        b_sq = [Buf(), Buf()]
        b_tmp = [Buf(), Buf(), Buf()]
        b_bcr = [Buf() for _ in range(NB)]
        b_bcm = [Buf() for _ in range(NB)]
        b_bcf = [Buf() for _ in range(NB)]
        b_ost = [Buf(), Buf()]
        b_par, b_id, b_ones, b_onesf, b_eps = Buf(), Buf(), Buf(), Buf(), Buf()
        b_diag = [Buf(), Buf()]
        b_dmy = Buf()
        final_toks = []

        def pc(col):
            return par[:, col:col + 1]

        slabs = []
        for g in range(3):
            for n in range(14):
                slabs.append((w8_d[n], 2048))
            for hf in range(2):
                for c in range(11):
                    slabs.append((w8_d[14 + hf * 11 + c], 2048))
                for dc in range(8):
                    slabs.append((wd_d[hf * 8 + dc], 1408))
        issued = [0]

        def prefetch(upto):
            while issued[0] <= min(upto, len(slabs) - 1):
                n = issued[0]
                src, ncol = slabs[n]
                S.dma(wsl[:, n % NSLOT, 0:ncol], src, writes=[b_w[n % NSLOT]], eng="pool",
                      reads=(first_x if n == 0 else []))
                issued[0] += 1

        cur = [0]
        first_x = []

        def next_slab():
            n = cur[0]
            cur[0] += 1
            prefetch(n + NSLOT - 1)
            return n % NSLOT

        S.dma(par[:, :], par_d[:, :], writes=[b_par])
        S.dma(idt[:, :], id_d[:, :], writes=[b_id])
        S.op("dve", lambda e: e.memset(ones_f[:, :], 1.0), writes=[b_onesf])
        S.op("dve", lambda e: e.memset(epsb[:, :], EPS), writes=[b_eps])
        S.op("dve", lambda e: e.tensor_copy(ones[:, :], ones_f[:, :]), reads=[b_onesf], writes=[b_ones])
        for j in range(4):
            S.op("dve", lambda e, j=j: e.memset(ubh[:, j, :].bitcast(F32), 0.0), writes=[b_ubh[j]])
            S.op("dve", lambda e, j=j: e.memset(cvh[:, j, :], 0.0), writes=[b_cvh[j]])
        def load_sample_hist():
            S.dma(ubs[:, :, :, :], ubs_d.rearrange("p (j s k) -> p j s k", j=4, s=NSEQ_S), writes=b_ubs)
            S.dma(cvs[:, :, :, :], cvs_d.rearrange("p (j s k) -> p j s k", j=4, s=NSEQ_S), writes=b_cvs)

        def sample_hist_conv(j, part):
            accs = [accS[:, a, j, :].rearrange("p (s t) -> p s t", t=TS) for a in range(2)]
            if part == 0:
                S.op("dve", lambda e: e.tensor_scalar(accs[0], ubs[:, j, :, 0:8], pc(P_CBW + j * KB + 0), pc(P_CBB + j), ALU.mult, ALU.add),
                     reads=[b_ubs[j], b_par], writes=[b_accS[0][j]])
                S.op("dve", lambda e: e.tensor_scalar(accs[1], ubs[:, j, :, 1:9], pc(P_CBW + j * KB + 1), None, ALU.mult),
                     reads=[b_ubs[j], b_par], writes=[b_accS[1][j]])
            ks = range(2, 16) if part == 0 else range(16, KB - 1)
            for k in ks:
                a = k % 2
                S.op("dve", lambda e, k=k, a=a: e.scalar_tensor_tensor(accs[a], ubs[:, j, :, k:k + 8], pc(P_CBW + j * KB + k), accs[a], ALU.mult, ALU.add),
                     reads=[b_ubs[j], b_par, b_accS[a][j]], writes=[b_accS[a][j]])
            if part == 1:
                S.op("dve", lambda e: e.tensor_tensor(accs[0], accs[0], accs[1], ALU.add),
                     reads=[b_accS[0][j], b_accS[1][j]], writes=[b_accS[0][j]])

        def blk_of(col, blocks):
            for bi_, (n0, n1) in enumerate(blocks):
                if n0 <= col < n1:
                    return bi_
            raise ValueError

        def xv(g):
            return X[:, g % 2], b_xx[g % 2]

        def stats_act(g, bi_, rt, b_rt):
            xt_, bx = xv(g)
            n0, n1 = GROUPS[g][3][bi_]
            N = n1 - n0
            pb = 6 + bi_
            for dc in range(8):
                s = dc % 2
                if s == 0 or bi_ > 0:
                    S.op("act", lambda e, dc=dc, s=s: e.activation(sq[:, s, 0:N], xt_[:, dc, n0:n1], AF.Square),
                         reads=[bx[dc][bi_]], writes=[b_sq[s]])
                else:
                    S.op("dve", lambda e, dc=dc, s=s: e.tensor_tensor(sq[:, s, 0:N], xt_[:, dc, n0:n1], xt_[:, dc, n0:n1], ALU.mult),
                         reads=[bx[dc][bi_]], writes=[b_sq[s]])
                S.mm_group_part(ps[:, pb, 0:N], ones[:, :], sq[:, s, 0:N], first=(dc == 0), last=(dc == 7),
                                reads=[b_ones, b_sq[s]], writes=[b_ps[pb]])
            stats_finish(pb, n0, n1, bi_, rt, b_rt)

        def preload_sqrt():
            S.op("act", lambda e: e.activation(dmy[:, 1:2], epsb[:, 0:1], AF.Sqrt), reads=[b_eps], writes=[b_dmy])

        def stats_sqrt(pb, n0, n1, bi_, rt, b_rt):
            N = n1 - n0
            S.op("act", lambda e: e.activation(rt[:, n0:n1], ps[:, pb, 0:N], AF.Sqrt, bias=epsb[:, 0:1], scale=1.0 / D),
                 reads=[b_ps[pb], b_eps], writes=[b_rt[bi_]])

        def stats_recip(n0, n1, bi_, rt, b_rt):
            S.op("dve", lambda e: e.reciprocal(rt[:, n0:n1], rt[:, n0:n1]),
                 reads=[b_rt[bi_]], writes=[b_rt[bi_]])

        def stats_finish(pb, n0, n1, bi_, rt, b_rt):
            stats_sqrt(pb, n0, n1, bi_, rt, b_rt)
            stats_recip(n0, n1, bi_, rt, b_rt)

        def stat_sq(g, dc):
            xt_, bx = xv(g)
            items = []
            for bi_, (n0, n1) in enumerate(GROUPS[g][3]):
                N = n1 - n0
                s = bi_ % 2
                S.op("act", lambda e, dc=dc, s=s, n0=n0, n1=n1, N=N: e.activation(sq[:, s, 0:N], xt_[:, dc, n0:n1], AF.Square),
                     reads=[bx[dc][bi_]], writes=[b_sq[s]])
                items.append((bi_, s, N))
            return items

        def stat_mm(items, first, last):
            for (bi_, s, N) in items:
                S.mm_group_part(ps[:, 6 + bi_, 0:N], ones[:, :], sq[:, s, 0:N], first=first, last=last,
                                reads=[b_ones, b_sq[s]], writes=[b_ps[6 + bi_]])

        def stat_fin(g, bi_, rt, b_rt):
            n0, n1 = GROUPS[g][3][bi_]
            stats_finish(6 + bi_, n0, n1, bi_, rt, b_rt)

        def apply_h(g, bi_, gcol, dcs=range(8)):
            xt_, bx = xv(g)
            n0, n1 = GROUPS[g][3][bi_]
            for dc in dcs:
                S.op("dve", lambda e, dc=dc: e.scalar_tensor_tensor(
                    hT[:, dc, n0:n1], xt_[:, dc, n0:n1], pc(gcol + dc), bcr[:, n0:n1], ALU.mult, ALU.mult),
                    reads=[bx[dc][bi_], b_par, b_bcr[bi_]], writes=[b_h[dc][bi_]])

        def apply_out(g, bi_, dcs=range(8)):
            xt_, bx = xv(g)
            p0, Tp, Tsm, blocks = GROUPS[g][0], GROUPS[g][1], GROUPS[g][2], GROUPS[g][3]
            n0, n1 = blocks[bi_]
            N = n1 - n0
            last = (g == len(GROUPS) - 1)
            ring = [(ost[:, 0, :], b_ost[0]), (ost[:, 1, :], b_ost[1])]
            if last:
                ring += [(tmp[:, i, :], b_tmp[i]) for i in range(3)]
            for dc in dcs:
                st, bst = ring[(bi_ * 8 + dc) % len(ring)]
                S.op("dve", lambda e, dc=dc, st=st: e.scalar_tensor_tensor(
                    st[:, 0:N], xt_[:, dc, n0:n1], pc(P_GFIN + dc), bcf[:, n0:n1], ALU.mult, ALU.mult),
                    reads=[bx[dc][bi_], b_par, b_bcf[bi_]], writes=[bst])
                npr = min(n1, Tp) - n0
                q = "pool" if (last and dc % 2) else "sp"
                if npr > 0:
                    final_toks.append(S.dma(ypT_d[dc * 128:(dc + 1) * 128, p0 + n0:p0 + n0 + npr], st[:, 0:npr], reads=[bst], eng=q))
                if n1 > Tp:
                    s0 = max(n0, Tp) - n0
                    final_toks.append(S.dma(ysT_d[dc * 128:(dc + 1) * 128, :], st[:, s0:s0 + 128], reads=[bst], eng=q))

        def load_x(g):
            p0, Tp, Tsm, blocks = GROUPS[g][0], GROUPS[g][1], GROUPS[g][2], GROUPS[g][3]
            xt_, bx = xv(g)
            xpv = xpT_d.rearrange("(dc p) t -> p dc t", p=128)
            for bi_, (n0, n1) in enumerate(blocks):
                m1 = min(n1, Tp)
                if m1 > n0:
                    for h in range(2):
                        S.dma(xt_[:, 4 * h:4 * h + 4, n0:m1], xpv[:, 4 * h:4 * h + 4, p0 + n0:p0 + m1],
                              writes=[bx[dc][bi_] for dc in range(4 * h, 4 * h + 4)])
                if n1 > Tp:
                    S.dma(xt_[:, :, Tp:Tp + Tsm], xsT_d.rearrange("(dc p) t -> p dc t", p=128),
                          writes=[bx[dc][bi_] for dc in range(8)])

        def mm_group_part(out_ap, l, r, first, last, reads, writes):
            waits = S._waits("pe", S._deps(reads, writes if first else []))
            S.cnt["pe"] += 1
            tok = ("pe", S.sem["pe"], S.cnt["pe"])

            def run(e, waits=waits, tok=tok):
                for sh, val in waits:
                    e.wait_ge(sh, val)
                e.matmul(out_ap, l, r, start=first, stop=last).then_inc(tok[1], 1)

            S.q["pe"].append(run)
            S._commit("pe", tok, reads, writes if last else [])
            return tok

        S.mm_group_part = mm_group_part

        def proj(slot, mloc, kcs, rhs_t, b_rhs, blocks, wk=128, mw=2, fine=False, only=None):
            res = []
            for bi_, (n0, n1) in enumerate(blocks):
                if only is not None and bi_ != only:
                    continue
                pb = psum()
                pairs = [(wsl[:, slot, (kc * mw + mloc) * 128:(kc * mw + mloc + 1) * 128], rhs_t[:, kc, n0:n1]) for kc in range(kcs)]
                if fine:
                    for kc, (l, r) in enumerate(pairs):
                        S.mm_group_part(ps[:, pb, 0:n1 - n0], l, r, first=(kc == 0), last=(kc == kcs - 1),
                                        reads=[b_w[slot], b_rhs[kc][bi_]], writes=[b_ps[pb]])
                else:
                    S.mm_group(ps[:, pb, 0:n1 - n0], pairs,
                               reads=[b_w[slot]] + [b_rhs[kc][bi_] for kc in range(kcs)], writes=[b_ps[pb]])
                res.append((bi_, n0, n1, pb))
            return res

        def do_group(g, p0, Tp, Tsm, blocks, cblocks, prev_final=None):
            Tg = Tp + Tsm
            last_group = (g == len(GROUPS) - 1)
            xg, bxg = xv(g)
            nblk = len(blocks)

            DHALF = [(0, 16), (16, KB)]

            def build_diag(j):
                for half in range(2):
                    S.dma(diag[:, half, :, :].rearrange("p k m -> p (k m)"), dgw_d[j * 2 + half], writes=[b_diag[half]], eng="pool")

            build_diag(0)
            NDVE = 7 if last_group else 8

            def conv_acc(j):
                if j % 2 == 0:
                    return bcm[:, 0:Tp], b_bcm
                return ost[:, :, :].rearrange("p a n -> p (a n)")[:, 0:Tp], b_ost

            def conv_b_dve(j):
                jb = j % 2
                acc, bacc = conv_acc(j)
                S.op("act", lambda e: e.activation(acc, ubp[:, jb, 0:Tp], AF.Identity, scale=pc(P_CBW + j * KB + 0)),
                     reads=[b_ubp[jb], b_par], writes=bacc)
                for k in range(1, NDVE):
                    S.op("dve", lambda e, k=k: e.scalar_tensor_tensor(acc, ubp[:, jb, k:k + Tp], pc(P_CBW + j * KB + k), acc, ALU.mult, ALU.add),
                         reads=[b_ubp[jb], b_par] + bacc, writes=bacc)

            def conv_b_prompt(j):
                jb = j % 2
                acc, bacc = conv_acc(j)
                pbs = [psum() for _ in cblocks]
                for half, (k0, k1) in enumerate(DHALF):
                    for k in range(max(k0, NDVE), k1):
                        for (c0, c1), pb in zip(cblocks, pbs):
                            S.mm_group_part(ps[:, pb, 0:c1 - c0], diag[:, half, k - k0, :], ubp[:, jb, k + c0:k + c1], first=(k == NDVE), last=(k == KB - 1),
                                            reads=[b_diag[half], b_ubp[jb]], writes=[b_ps[pb]])
                if j < 3:
                    build_diag(j + 1)
                for (c0, c1), pb in zip(cblocks, pbs):
                    S.op("dve", lambda e, pb=pb, c0=c0, c1=c1: e.scalar_tensor_tensor(cbT[:, j, c0:c1], ps[:, pb, 0:c1 - c0], pc(P_CBB + j), acc[:, c0:c1], ALU.add, ALU.add),
                         reads=[b_ps[pb], b_par] + bacc, writes=b_y[4 + j])
                if not last_group:
                    S.op("dve", lambda e, j=j, jb=jb: e.tensor_copy(ubh[:, j, :], ubp[:, jb, Tp:Tp + 30]),
                         reads=[b_ubp[jb]], writes=[b_ubh[j]])
                else:
                    final_toks.append(S.dma(nbp_o[:, j * 30:(j + 1) * 30], ubp[:, jb, Tp:Tp + 30].bitcast(F32), reads=[b_ubp[jb]]))

            def conv_b_sample(j):
                acc = accS[:, 0, j, :].rearrange("p (s t) -> p s t", t=TS)
                accr = cbT[:, j, Tp:Tp + 128].rearrange("p (s t) -> p s t", t=TS)
                for k in range(KB - TS, KB):
                    d = KB - 1 - k
                    lastk = (k == KB - 1)
                    o = accr if lastk else acc
                    S.op("dve", lambda e, k=k, d=d, o=o: e.scalar_tensor_tensor(o[:, :, d:TS], ubs[:, j, :, 30:30 + TS - d], pc(P_CBW + j * KB + k), acc[:, :, d:TS], ALU.mult, ALU.add),
                         reads=[b_ubs[j], b_par, b_accS[0][j]], writes=(b_y[4 + j] if lastk else [b_accS[0][j]]))

            pending = []
            for j in range(4):
                slot = next_slab()
                jb = j % 2
                S.op("dve", lambda e, j=j, jb=jb: e.tensor_copy(ubp[:, jb, 0:30], ubh[:, j, :]),
                     reads=[b_ubh[j]], writes=[b_ubp[jb]])
                if j == 0:
                    rv, rg = [], []
                    for ob in range(nblk):
                        rv += proj(slot, 0, 8, hT, b_h, blocks, fine=True, only=ob)
                        rg += proj(slot, 1, 8, hT, b_h, blocks, only=ob)
                else:
                    rv = proj(slot, 0, 8, hT, b_h, blocks)
                    rg = proj(slot, 1, 8, hT, b_h, blocks)
                for (bi_, n0, n1, pv), (_, _, _, pg) in zip(rv, rg):
                    N = n1 - n0
                    ts_ = bi_ % 3
                    S.op("act", lambda e, pg=pg, N=N, ts_=ts_: e.activation(tmp[:, ts_, 0:N], ps[:, pg, 0:N], AF.Sigmoid),
                         reads=[b_ps[pg]], writes=[b_tmp[ts_]])
                    npr = min(n1, Tp) - n0
                    if npr > 0:
                        S.op("dve", lambda e, pv=pv, ts_=ts_, jb=jb, n0=n0, npr=npr: e.tensor_tensor(ubp[:, jb, 30 + n0:30 + n0 + npr], ps[:, pv, 0:npr], tmp[:, ts_, 0:npr], ALU.mult),
                             reads=[b_ps[pv], b_tmp[ts_]], writes=[b_ubp[jb]])
                    if n1 > Tp:
                        s0 = max(n0, Tp) - n0
                        S.op("dve", lambda e, pv=pv, ts_=ts_, j=j, s0=s0: e.tensor_tensor(
                            ubs[:, j, :, 30:38], ps[:, pv, s0:s0 + 128].rearrange("p (s t) -> p s t", t=TS),
                            tmp[:, ts_, s0:s0 + 128].rearrange("p (s t) -> p s t", t=TS), ALU.mult),
                            reads=[b_ps[pv], b_tmp[ts_]], writes=[b_ubs[j]])
                conv_b_dve(j)
                if pending:
                    pending.pop(0)()
                pending.append(lambda j=j: (conv_b_prompt(j), conv_b_sample(j) if Tsm else None))
                if j == 3:
                    preload_sqrt()

            def ln_steps():
                p1, p2 = 6, 7

                def sqs(bi_, js):
                    n0, n1 = blocks[bi_]
                    N = n1 - n0
                    for j in js:
                        S.op("act", lambda e, j=j, s=j % 2: e.activation(sq[:, s, 0:N], cbT[:, j, n0:n1], AF.Square),
                             reads=[b_y[4 + j][bi_]], writes=[b_sq[j % 2]])

                def mms(bi_, js):
                    n0, n1 = blocks[bi_]
                    N = n1 - n0
                    for j in js:
                        S.mm_group_part(ps[:, p1, 0:N], ones[:, :], cbT[:, j, n0:n1], first=(j == 0), last=(j == 3),
                                        reads=[b_ones, b_y[4 + j][bi_]], writes=[b_ps[p1]])
                        S.mm_group_part(ps[:, p2, 0:N], ones[:, :], sq[:, j % 2, 0:N], first=(j == 0), last=(j == 3),
                                        reads=[b_ones, b_sq[j % 2]], writes=[b_ps[p2]])

                def fin(bi_):
                    n0, n1 = blocks[bi_]
                    N = n1 - n0
                    S.op("act", lambda e: e.activation(bcm[:, n0:n1], ps[:, p1, 0:N], AF.Copy, scale=1.0 / GW),
                         reads=[b_ps[p1]], writes=[b_bcm[bi_]])
                    S.op("act", lambda e: e.activation(tmp[:, 2, 0:N], bcm[:, n0:n1], AF.Square),
                         reads=[b_bcm[bi_]], writes=[b_tmp[2]])
                    S.op("dve", lambda e: e.scalar_tensor_tensor(tmp[:, 2, 0:N], ps[:, p2, 0:N], 1.0 / GW, tmp[:, 2, 0:N], ALU.mult, ALU.subtract),
                         reads=[b_ps[p2], b_tmp[2]], writes=[b_tmp[2]])
                    S.op("act", lambda e: e.activation(bcr[:, n0:n1], tmp[:, 2, 0:N], AF.Sqrt, bias=epsb[:, 0:1], scale=1.0),
                         reads=[b_tmp[2], b_eps], writes=[b_bcr[bi_]])
                    S.op("dve", lambda e: e.reciprocal(bcr[:, n0:n1], bcr[:, n0:n1]),
                         reads=[b_bcr[bi_]], writes=[b_bcr[bi_]])

                return [
                    lambda: sqs(0, (0, 1)),
                    lambda: (mms(0, (0, 1)), sqs(0, (2, 3))),
                    lambda: (mms(0, (2, 3)), sqs(1, (0, 1)), fin(0)),
                    lambda: (mms(1, (0, 1)), sqs(1, (2, 3))),
                    lambda: (mms(1, (2, 3)), fin(1)),
                ]

            def ln_part2(bi_):
                n0, n1 = blocks[bi_]
                N = n1 - n0
                for j in (0, 2, 1, 3):
                    if j < 2:
                        eng, scr, bscr = "dve", tmp[:, j % 2, 0:N], b_tmp[j % 2]
                    else:
                        eng, scr, bscr = "pool", ost[:, j % 2, 0:N], b_ost[j % 2]
                    S.op(eng, lambda e, j=j, scr=scr: e.tensor_tensor(scr, cbT[:, j, n0:n1], bcm[:, n0:n1], ALU.subtract),
                         reads=[b_y[4 + j][bi_], b_bcm[bi_]], writes=[bscr])
                    S.op(eng, lambda e, scr=scr: e.tensor_tensor(scr, scr, bcr[:, n0:n1], ALU.mult),
                         reads=[bscr, b_bcr[bi_]], writes=[bscr])
                    S.op("act", lambda e, j=j, scr=scr: e.activation(yfT[:, 4 + j, n0:n1], scr, AF.Silu, bias=pc(P_LNB + j), scale=pc(P_LNG + j)),
                         reads=[bscr, b_par], writes=[b_y[4 + j][bi_]])

            for j in range(4):
                slot = next_slab()
                jb = j % 2
                S.op("dve", lambda e, j=j, jb=jb: e.tensor_copy(cvp[:, jb, 0:2], cvh[:, j, :]),
                     reads=[b_cvh[j]], writes=[b_cvp[jb]])
                rc = proj(slot, 0, 8, hT, b_h, blocks)
                rv = proj(slot, 1, 8, hT, b_h, blocks)
                for (bi_, n0, n1, pcg), (_, _, _, pv) in zip(rc, rv):
                    N = n1 - n0
                    ts_ = bi_ % 2
                    S.op("act", lambda e, pcg=pcg, N=N, ts_=ts_: e.activation(tmp[:, ts_, 0:N], ps[:, pcg, 0:N], AF.Copy),
                         reads=[b_ps[pcg]], writes=[b_tmp[ts_]])
                    npr = min(n1, Tp) - n0
                    if npr > 0:
                        S.op("dve", lambda e, pv=pv, ts_=ts_, jb=jb, n0=n0, npr=npr: e.tensor_tensor(cvp[:, jb, 2 + n0:2 + n0 + npr], ps[:, pv, 0:npr], tmp[:, ts_, 0:npr], ALU.mult),
                             reads=[b_ps[pv], b_tmp[ts_]], writes=[b_cvp[jb]])
                    if n1 > Tp:
                        s0 = max(n0, Tp) - n0
                        S.op("dve", lambda e, pv=pv, ts_=ts_, j=j, s0=s0: e.tensor_tensor(
                            cvs[:, j, :, 2:10], ps[:, pv, s0:s0 + 128].rearrange("p (s t) -> p s t", t=TS),
                            tmp[:, ts_, s0:s0 + 128].rearrange("p (s t) -> p s t", t=TS), ALU.mult),
                            reads=[b_ps[pv], b_tmp[ts_]], writes=[b_cvs[j]])
                S.op("act", lambda e, j=j, jb=jb: e.activation(caT[:, j, 0:Tp], cvp[:, jb, 2:2 + Tp], AF.Identity, scale=pc(P_CAW + j * 3 + 2)),
                     reads=[b_cvp[jb], b_par], writes=b_y[j])
                for k in range(2):
                    S.op("dve", lambda e, j=j, jb=jb, k=k: e.scalar_tensor_tensor(caT[:, j, 0:Tp], cvp[:, jb, k:k + Tp], pc(P_CAW + j * 3 + k), caT[:, j, 0:Tp], ALU.mult, ALU.add),
                         reads=[b_cvp[jb], b_par] + b_y[j], writes=b_y[j])
                if Tsm:
                    accs = caT[:, j, Tp:Tp + 128].rearrange("p (s t) -> p s t", t=TS)
                    S.op("dve", lambda e, j=j, accs=accs: e.tensor_scalar(accs, cvs[:, j, :, 2:10], pc(P_CAW + j * 3 + 2), None, ALU.mult),
                         reads=[b_cvs[j], b_par], writes=b_y[j])
                    for k in range(2):
                        S.op("dve", lambda e, j=j, k=k, accs=accs: e.scalar_tensor_tensor(accs, cvs[:, j, :, k:k + 8], pc(P_CAW + j * 3 + k), accs, ALU.mult, ALU.add),
                             reads=[b_cvs[j], b_par] + b_y[j], writes=b_y[j])
                if not last_group:
                    S.op("dve", lambda e, j=j, jb=jb: e.tensor_copy(cvh[:, j, :], cvp[:, jb, Tp:Tp + 2]),
                         reads=[b_cvp[jb]], writes=[b_cvh[j]])
                else:
                    final_toks.append(S.dma(nap_o[:, j * 2:(j + 1) * 2], cvp[:, jb, Tp:Tp + 2], reads=[b_cvp[jb]]))
                if pending:
                    pending.pop(0)()
                if j == 0:
                    lnq = ln_steps()
                    lnq.pop(0)()
                elif j == 1:
                    lnq.pop(0)()
                    lnq.pop(0)()
                elif j == 2:
                    lnq.pop(0)()
                    lnq.pop(0)()
                    ln_part2(0)
                else:
                    ln_part2(1)
            while pending:
                pending.pop(0)()
            while lnq:
                lnq.pop(0)()
            preload_sqrt()
            for jj in range(2):
                slot = next_slab()
                for m in range(2):
                    j = 2 * jj + m
                    for (bi_, n0, n1, pb) in proj(slot, m, 8, hT, b_h, blocks):
                        N = n1 - n0
                        S.op("dve", lambda e, pb=pb, j=j, n0=n0, n1=n1, N=N: e.tensor_tensor(yfT[:, j, n0:n1], ps[:, pb, 0:N], caT[:, j, n0:n1], ALU.mult),
                             reads=[b_ps[pb], b_y[j][bi_]], writes=[b_y[j][bi_]])
            if last_group:
                final_toks.append(S.dma(ubs_o.rearrange("p (j s k) -> p j s k", j=4, s=NSEQ_S), ubs[:, :, :, :], reads=b_ubs))
                final_toks.append(S.dma(cvs_o.rearrange("p (j s k) -> p j s k", j=4, s=NSEQ_S), cvs[:, :, :, :], reads=b_cvs))

            pend = None
            for sl in range(4):
                slot = next_slab()
                for m in range(2):
                    dc = 2 * sl + m
                    for (bi_, n0, n1, pb) in proj(slot, m, 8, yfT, b_y, blocks):
                        N = n1 - n0
                        S.op("dve", lambda e, pb=pb, dc=dc, n0=n0, n1=n1, N=N: e.tensor_tensor(xg[:, dc, n0:n1], xg[:, dc, n0:n1], ps[:, pb, 0:N], ALU.add),
                             reads=[b_ps[pb], bxg[dc][bi_]], writes=[bxg[dc][bi_]])
                    if pend is not None:
                        stat_mm(pend[1], first=(pend[0] == 0), last=False)
                    pend = (dc, stat_sq(g, dc))
            stat_mm(pend[1], first=False, last=True)
            for bi_ in range(nblk):
                stat_fin(g, bi_, bcr, b_bcr)
                apply_h(g, bi_, P_GFFN)
            for hf in range(2):
                hooks = []
                if hf == 0:
                    hooks = list(prev_final or [])
                    if g == 0:
                        hooks.append(load_sample_hist)
                    if not last_group:
                        hooks.append(lambda: load_x(g + 1))
                n1pend = None
                if hf == 1 and g == 0:
                    hooks = [lambda j=j, part=part: sample_hist_conv(j, part) for j in range(4) for part in range(2)]
                for c in range(11):
                    if c >= 1 and hooks:
                        hooks.pop(0)()
                    if hf == 1 and not last_group and c < 9:
                        if n1pend is not None:
                            stat_mm(n1pend[1], first=(n1pend[0] == 0), last=(n1pend[0] == 7))
                        n1pend = (c, stat_sq(g + 1, c)) if c < 8 else None
                    slot = next_slab()
                    if hf == 0 and c == 0:
                        rg, ru = [], []
                        for ob in range(nblk):
                            rg += proj(slot, 0, 8, hT, b_h, blocks, fine=True, only=ob)
                            ru += proj(slot, 1, 8, hT, b_h, blocks, only=ob)
                    else:
                        rg = proj(slot, 0, 8, hT, b_h, blocks)
                        ru = proj(slot, 1, 8, hT, b_h, blocks)
                    for (bi_, n0, n1, pg), (_, _, _, pu) in zip(rg, ru):
                        N = n1 - n0
                        ts_ = (c * 2 + bi_) % 3
                        S.op("act", lambda e, pg=pg, N=N, ts_=ts_: e.activation(tmp[:, ts_, 0:N], ps[:, pg, 0:N], AF.Silu),
                             reads=[b_ps[pg]], writes=[b_tmp[ts_]])
                        S.op("dve", lambda e, pu=pu, ts_=ts_, c=c, n0=n0, n1=n1, N=N: e.tensor_tensor(yfT[:, c, n0:n1], ps[:, pu, 0:N], tmp[:, ts_, 0:N], ALU.mult),
                             reads=[b_ps[pu], b_tmp[ts_]], writes=[b_y[c][bi_]])
                n1q = []
                if hf == 1 and not last_group:
                    nb = GROUPS[g + 1][3]
                    n1q = [lambda: (stats_sqrt(6, nb[0][0], nb[0][1], 0, bcr, b_bcr), stats_sqrt(7, nb[1][0], nb[1][1], 1, bcr, b_bcr),
                                    stats_recip(nb[0][0], nb[0][1], 0, bcr, b_bcr))]
                    n1q += [lambda d0=d0: apply_h(g + 1, 0, P_GMIX, range(d0, d0 + 2)) for d0 in range(0, 8, 2)]
                    n1q += [lambda: (stats_recip(nb[1][0], nb[1][1], 1, bcr, b_bcr), apply_h(g + 1, 1, P_GMIX, range(0, 3))),
                            lambda: apply_h(g + 1, 1, P_GMIX, range(3, 6)), lambda: apply_h(g + 1, 1, P_GMIX, range(6, 8))]
                pend = None
                if hf == 1 and last_group:
                    preload_sqrt()
                for dc in range(8):
                    if n1q:
                        n1q.pop(0)()
                    slot = next_slab()
                    for (bi_, n0, n1, pb) in proj(slot, 0, 11, yfT, b_y, blocks, mw=1):
                        N = n1 - n0
                        S.op("dve", lambda e, pb=pb, dc=dc, n0=n0, n1=n1, N=N: e.tensor_tensor(xg[:, dc, n0:n1], xg[:, dc, n0:n1], ps[:, pb, 0:N], ALU.add),
                             reads=[b_ps[pb], bxg[dc][bi_]], writes=[bxg[dc][bi_]])
                    if hf == 1:
                        if pend is not None:
                            stat_mm(pend[1], first=(pend[0] == 0), last=False)
                        pend = (dc, stat_sq(g, dc))
                if hf == 1:
                    stat_mm(pend[1], first=False, last=True)
                    for bi_ in range(nblk):
                        stat_fin(g, bi_, bcf, b_bcf)
            return [lambda bi_=bi_, d0=d0: apply_out(g, bi_, range(d0, d0 + 2)) for bi_ in range(nblk) for d0 in range(0, 8, 2)]

        load_x(0)
        first_x.extend(b_xx[0][dc][0] for dc in range(8))
        prefetch(NSLOT - 2)
        for bi_ in range(len(GROUPS[0][3])):
            stats_act(0, bi_, bcr, b_bcr)
            apply_h(0, bi_, P_GMIX)
        deferred = None
        for g, grp in enumerate(GROUPS):
            deferred = do_group(g, *grp, prev_final=deferred)
        for st in deferred:
            st()

        S.finish(final_toks)
    return nc


def _prep_weights(w_in, w_out, w_gate, w_up, w_down):
    def slab(cols_src, kc):
        return np.ascontiguousarray(cols_src.reshape(kc, 128, -1).transpose(1, 0, 2)).reshape(128, -1)

    w_in = w_in[0]
    cc = lambda a, i: a[:, i * 128:(i + 1) * 128]
    B, C, V, GV, GG = 0, 4, 8, 12, 16
    w8 = np.empty((36, 128, 2048), np.float32)
    n = 0
    for j in range(4):
        w8[n] = slab(np.concatenate([cc(w_in, GV + j), cc(w_in, GG + j)], 1), 8); n += 1
    for j in range(4):
        w8[n] = slab(np.concatenate([cc(w_in, C + j), cc(w_in, V + j)], 1), 8); n += 1
    for jj in range(2):
        w8[n] = slab(np.concatenate([cc(w_in, B + 2 * jj), cc(w_in, B + 2 * jj + 1)], 1), 8); n += 1
    wo = w_out[0]
    for sl in range(4):
        w8[n] = slab(wo[:, sl * 256:(sl + 1) * 256], 8); n += 1
    wg, wu = w_gate[0], w_up[0]
    for c in range(22):
        w8[n] = slab(np.concatenate([cc(wg, c), cc(wu, c)], 1), 8); n += 1
    assert n == 36
    wdn = w_down[0]
    wd = np.empty((16, 128, 1408), np.float32)
    for hf in range(2):
        for dc in range(8):
            wd[hf * 8 + dc] = slab(wdn[hf * 1408:(hf + 1) * 1408, dc * 128:(dc + 1) * 128], 11)
    return w8, wd


def _prep_params(g_mix, g_ffn, g_final, conv_a_w, conv_b_w, conv_b_bias, ln_b_g, ln_b_b):
    p = np.empty((128, NPARAM), np.float32)
    p[:, P_GMIX:P_GMIX + 8] = g_mix[0].reshape(8, 128).T
    p[:, P_GFFN:P_GFFN + 8] = g_ffn[0].reshape(8, 128).T
    p[:, P_GFIN:P_GFIN + 8] = g_final.reshape(8, 128).T
    p[:, P_CAW:P_CAW + 12] = conv_a_w[0].reshape(3, 4, 128).transpose(2, 1, 0).reshape(128, 12)
    p[:, P_CBW:P_CBW + 124] = conv_b_w[0].reshape(31, 4, 128).transpose(2, 1, 0).reshape(128, 124)
    p[:, P_CBB:P_CBB + 4] = conv_b_bias[0].reshape(4, 128).T
    p[:, P_LNG:P_LNG + 4] = ln_b_g[0].reshape(4, 128).T
    p[:, P_LNB:P_LNB + 4] = ln_b_b[0].reshape(4, 128).T
    return p


_NC_CACHE = {}


def _make_in_maps(x_prompt, x_sample, state_conv_a, state_conv_b, g_mix, w_in, conv_a_w, conv_b_w,
                  conv_b_bias, ln_b_g, ln_b_b, w_out, g_ffn, w_gate, w_up, w_down, g_final, cores=None):
    f = lambda a: np.ascontiguousarray(np.asarray(a, dtype=np.float32))
    x_prompt, x_sample, state_conv_a, state_conv_b = f(x_prompt), f(x_sample), f(state_conv_a), f(state_conv_b)
    w8, wd = _prep_weights(f(w_in), f(w_out), f(w_gate), f(w_up), f(w_down))
    params = _prep_params(f(g_mix), f(g_ffn), f(g_final), f(conv_a_w), f(conv_b_w), f(conv_b_bias), f(ln_b_g), f(ln_b_b))
    ident = np.eye(128, dtype=np.float32)
    cbw = f(conv_b_w)[0].reshape(KB, 4, 128)
    dgw = np.zeros((4, 2, 128, 16, 128), np.float32)
    ar = np.arange(128)
    for j in range(4):
        for k in range(KB):
            dgw[j, k // 16, ar, k % 16, ar] = cbw[k, j]
    dgw = dgw.reshape(8, 128, 2048)
    in_maps = []
    for c in (range(NCORES) if cores is None else cores):
        sl = slice(c * NSEQ_S, (c + 1) * NSEQ_S)
        hb = state_conv_b[0, sl].reshape(NSEQ_S, KB - 1, 4, 128).transpose(3, 2, 0, 1)
        ubs_in = np.zeros((128, 4, NSEQ_S, KB - 1 + TS), np.float32)
        ubs_in[..., :KB - 1] = hb
        ha = state_conv_a[0, sl].reshape(NSEQ_S, KA - 1, 4, 128).transpose(3, 2, 0, 1)
        cvs_in = np.zeros((128, 4, NSEQ_S, KA - 1 + TS), np.float32)
        cvs_in[..., :KA - 1] = ha
        in_maps.append({
            "xpT": np.ascontiguousarray(x_prompt[c].T),
            "xsT": np.ascontiguousarray(x_sample[sl].reshape(NSEQ_S * TS, D).T),
            "ubs_in": ubs_in.reshape(128, -1), "cvs_in": cvs_in.reshape(128, -1),
            "w8": w8, "wd": wd, "params": params, "ident": ident, "dgw": dgw,
        })
    return in_maps


def _assemble_core(r):
    yp = np.ascontiguousarray(r["ypT"].T)
    ys = np.ascontiguousarray(r["ysT"].T).reshape(NSEQ_S, TS, D)
    nbs = r["ubs_out"].reshape(128, 4, NSEQ_S, KB - 1 + TS)[:, :, :, TS:].transpose(2, 3, 1, 0).reshape(NSEQ_S, KB - 1, GW)
    nas = r["cvs_out"].reshape(128, 4, NSEQ_S, KA - 1 + TS)[:, :, :, TS:].transpose(2, 3, 1, 0).reshape(NSEQ_S, KA - 1, GW)
    nbp = r["nbp_out"].reshape(128, 4, KB - 1).transpose(2, 1, 0).reshape(KB - 1, GW)
    nap = r["nap_out"].reshape(128, 4, KA - 1).transpose(2, 1, 0).reshape(KA - 1, GW)
    return yp, ys, nap, nbp, nas, nbs


def kernel(**inputs):
    in_maps = _make_in_maps(**inputs)
    if "nc" not in _NC_CACHE:
        _NC_CACHE["nc"] = build_nc()
    nc = _NC_CACHE["nc"]
    res = run_bass_kernel_spmd(nc, in_maps, core_ids=list(range(NCORES)))
    outs = [_assemble_core(r) for r in res.results]
    f32 = lambda a: np.ascontiguousarray(a, dtype=np.float32)
    y_prompt = f32(np.stack([o[0] for o in outs], 0))
    y_sample = f32(np.concatenate([o[1] for o in outs], 0))
    nap = f32(np.stack([o[2] for o in outs], 0)[None])
    nbp = f32(np.stack([o[3] for o in outs], 0)[None])
    nas = f32(np.concatenate([o[4] for o in outs], 0)[None])
    nbs = f32(np.concatenate([o[5] for o in outs], 0)[None])
    return (y_prompt, y_sample, nap, nbp, nas, nbs)
```

```python
import numpy as np
from contextlib import ExitStack
import concourse.bass as bass
import concourse.mybir as mybir
from concourse.bass_utils import run_bass_kernel_spmd

F32 = mybir.dt.float32
F32R = mybir.dt.float32r
AF = mybir.ActivationFunctionType
ALU = mybir.AluOpType

NCORES = 8
D = 1024
GW = 512
DFF = 2816
SEQ = 2048
NSEQ_S = 16
TS = 8
KA = 3
KB = 31
EPS = 1e-6
TGMAX = 768
NSLOT = 4

P_GMIX = 0
P_GFFN = 8
P_GFIN = 16
P_CAW = 24
P_CBW = 36
P_CBB = 160
P_LNG = 164
P_LNB = 168
NPARAM = 172

GROUPS = [
    (0, 768, 0, [(0, 512), (512, 768)], [(0, 512), (512, 768)]),
    (768, 768, 0, [(0, 512), (512, 768)], [(0, 512), (512, 768)]),
    (1536, 512, 128, [(0, 384), (384, 640)], [(0, 256), (256, 512)]),
]


class Buf:
    __slots__ = ("w", "r", "name")

    def __init__(self, name=""):
        self.w = None
        self.r = {}
        self.name = name


class Sched:
    ENGS = ["pe", "act", "dve", "pool", "sp"]

    def __init__(self, nc, es, n_dma_sems=12):
        self.nc = nc
        self.q = {e: [] for e in self.ENGS}
        self.sem = {e: es.enter_context(nc.semaphore("s_" + e)) for e in ["pe", "act", "dve", "pool"]}
        self.cnt = {e: 0 for e in self.sem}
        self.waited = {e: {} for e in self.ENGS}
        self.dsem = {q: [es.enter_context(nc.semaphore("d%s%d" % (q, i))) for i in range(n_dma_sems)] for q in ("sp", "pool")}
        self.dcnt = {q: [0] * n_dma_sems for q in ("sp", "pool")}
        self.dnext = {"sp": 0, "pool": 0}
        self.nwait = 0

    def _waits(self, eng, deps):
        waits = []
        for tok in deps:
            if tok is None:
                continue
            key, sh, val = tok
            if self.waited[eng].get(key, 0) >= val:
                continue
            self.waited[eng][key] = val
            waits.append((sh, val))
        self.nwait += len(waits)
        return waits

    @staticmethod
    def _deps(reads, writes):
        deps = []
        for b in reads:
            deps.append(b.w)
        for b in writes:
            deps.append(b.w)
            deps.extend(b.r.values())
        return deps

    @staticmethod
    def _commit(eng, tok, reads, writes):
        for b in reads:
            b.r[tok[0]] = tok
        for b in writes:
            b.w = tok
            b.r = {}

    def op(self, eng, fn, reads=(), writes=(), extra=()):
        waits = self._waits(eng, self._deps(reads, writes) + list(extra))
        self.cnt[eng] += 1
        tok = (eng, self.sem[eng], self.cnt[eng])

        def run(e, waits=waits, fn=fn, tok=tok):
            for sh, val in waits:
                e.wait_ge(sh, val)
            fn(e).then_inc(tok[1], 1)

        self.q[eng].append(run)
        self._commit(eng, tok, reads, writes)
        return tok

    def mm_group(self, out_ap, pairs, reads, writes):
        n = len(pairs)
        waits = self._waits("pe", self._deps(reads, writes))
        self.cnt["pe"] += 1
        tok = ("pe", self.sem["pe"], self.cnt["pe"])

        def run(e, waits=waits, pairs=pairs, tok=tok, out_ap=out_ap):
            for sh, val in waits:
                e.wait_ge(sh, val)
            ins = None
            for i, (l, r) in enumerate(pairs):
                ins = e.matmul(out_ap, l, r, start=(i == 0), stop=(i == n - 1))
            ins.then_inc(tok[1], 1)

        self.q["pe"].append(run)
        self._commit("pe", tok, reads, writes)
        return tok

    def dma(self, out_ap, in_ap, reads=(), writes=(), eng="sp"):
        i = self.dnext[eng]
        dsem, dcnt = self.dsem[eng], self.dcnt[eng]
        self.dnext[eng] = (i + 1) % len(dsem)
        key = "d%s%d" % (eng, i)
        prev = (key, dsem[i], 16 * dcnt[i]) if dcnt[i] else None
        waits = self._waits(eng, self._deps(reads, writes) + [prev])
        dcnt[i] += 1
        tok = (key, dsem[i], 16 * dcnt[i])

        def run(e, waits=waits, tok=tok):
            for sh, val in waits:
                e.wait_ge(sh, val)
            e.dma_start(out=out_ap, in_=in_ap).then_inc(tok[1], 16)

        self.q[eng].append(run)
        self._commit(eng, tok, reads, writes)
        return tok

    def finish(self, final_toks):
        nc = self.nc
        waits = self._waits("sp", final_toks)

        def fin(e, waits=waits):
            for sh, val in waits:
                e.wait_ge(sh, val)

        self.q["sp"].append(fin)
        q = self.q
        with nc.Block() as block:
            @block.tensor
            def _(e):
                for f in q["pe"]:
                    f(e)

            @block.scalar
            def _(e):
                for f in q["act"]:
                    f(e)

            @block.vector
            def _(e):
                for f in q["dve"]:
                    f(e)

            @block.gpsimd
            def _(e):
                for f in q["pool"]:
                    f(e)

            @block.sync
            def _(e):
                for f in q["sp"]:
                    f(e)


def build_nc():
    nc = bass.Bass("TRN2", target_bir_lowering=False)
    dt_in = lambda n, s: nc.dram_tensor(n, s, F32, kind="ExternalInput").ap()
    dt_out = lambda n, s: nc.dram_tensor(n, s, F32, kind="ExternalOutput").ap()
    xpT_d = dt_in("xpT", [D, SEQ])
    xsT_d = dt_in("xsT", [D, NSEQ_S * TS])
    ubs_d = dt_in("ubs_in", [128, 4 * NSEQ_S * (30 + TS)])
    cvs_d = dt_in("cvs_in", [128, 4 * NSEQ_S * (2 + TS)])
    w8_d = dt_in("w8", [36, 128, 2048])
    wd_d = dt_in("wd", [16, 128, 1408])
    par_d = dt_in("params", [128, NPARAM])
    id_d = dt_in("ident", [128, 128])
    dgw_d = dt_in("dgw", [8, 128, 2048])
    ypT_d = dt_out("ypT", [D, SEQ])
    ysT_d = dt_out("ysT", [D, NSEQ_S * TS])
    ubs_o = dt_out("ubs_out", [128, 4 * NSEQ_S * (30 + TS)])
    cvs_o = dt_out("cvs_out", [128, 4 * NSEQ_S * (2 + TS)])
    nbp_o = dt_out("nbp_out", [128, 4 * 30])
    nap_o = dt_out("nap_out", [128, 4 * 2])

    with ExitStack() as es:
        S = Sched(nc, es)
        sb = lambda n, s, d=F32: es.enter_context(nc.sbuf_tensor(n, s, d))
        X = sb("X", [128, 2, 8, TGMAX])
        hT = sb("hT", [128, 8, TGMAX], F32R)
        yfT = sb("yfT", [128, 11, TGMAX], F32R)
        cbT = yfT[:, 4:8, :]
        caT = yfT[:, 0:4, :]
        accS = sb("accS", [128, 2, 4, NSEQ_S * TS])
        wsl = sb("wsl", [128, NSLOT, 2048], F32R)
        cvp = sb("cvp", [128, 2, 2 + TGMAX])
        cvs = sb("cvs", [128, 4, NSEQ_S, 2 + TS])
        cvh = sb("cvh", [128, 4, 2])
        ubp = sb("ubp", [128, 2, 30 + TGMAX], F32R)
        ubs = sb("ubs", [128, 4, NSEQ_S, 30 + TS])
        ubh = sb("ubh", [128, 4, 30], F32R)
        sq = sb("sq", [128, 2, 512], F32R)
        tmp = sb("tmp", [128, 3, 512])
        bcr = sb("bcr", [128, TGMAX])
        bcm = sb("bcm", [128, TGMAX])
        bcf = sb("bcf", [128, TGMAX])
        ost = sb("ost", [128, 2, 512])
        par = sb("par", [128, NPARAM])
        idt = sb("idt", [128, 128])
        ones_f = sb("ones_f", [128, 128])
        ones = sb("ones", [128, 128], F32R)
        epsb = sb("epsb", [128, 1])
        dmy = sb("dmy", [128, 2])
        diag = sb("diag", [128, 2, 16, 128], F32R)
        ps = es.enter_context(nc.psum_tensor("ps", [128, 8, 512], F32))

        b_ps = [Buf("ps%d" % i) for i in range(8)]
        ps_next = [0]

        def psum():
            i = ps_next[0]
            ps_next[0] = (i + 1) % 6
            return i

        NB = 2
        b_xx = [[[Buf() for _ in range(NB)] for _ in range(8)] for _ in range(2)]
        b_h = [[Buf() for _ in range(NB)] for _ in range(8)]
        b_y = [[Buf() for _ in range(NB)] for _ in range(11)]
        b_accS = [[Buf() for _ in range(4)] for _ in range(2)]
        b_w = [Buf() for _ in range(NSLOT)]
        b_cvp = [Buf(), Buf()]
        b_cvs = [Buf() for _ in range(4)]
        b_cvh = [Buf() for _ in range(4)]
        b_ubp = [Buf(), Buf()]
        b_ubs = [Buf() for _ in range(4)]
        b_ubh = [Buf() for _ in range(4)]
        b_sq = [Buf(), Buf()]
        b_tmp = [Buf(), Buf(), Buf()]
        b_bcr = [Buf() for _ in range(NB)]
        b_bcm = [Buf() for _ in range(NB)]
        b_bcf = [Buf() for _ in range(NB)]
        b_ost = [Buf(), Buf()]
        b_par, b_id, b_ones, b_onesf, b_eps = Buf(), Buf(), Buf(), Buf(), Buf()
        b_diag = [Buf(), Buf()]
        b_dmy = Buf()
        final_toks = []

        def pc(col):
            return par[:, col:col + 1]

        slabs = []
        for g in range(3):
            for n in range(14):
                slabs.append((w8_d[n], 2048))
            for hf in range(2):
                for c in range(11):
                    slabs.append((w8_d[14 + hf * 11 + c], 2048))
                for dc in range(8):
                    slabs.append((wd_d[hf * 8 + dc], 1408))
        issued = [0]

        def prefetch(upto):
            while issued[0] <= min(upto, len(slabs) - 1):
                n = issued[0]
                src, ncol = slabs[n]
                S.dma(wsl[:, n % NSLOT, 0:ncol], src, writes=[b_w[n % NSLOT]], eng="pool",
                      reads=(first_x if n == 0 else []))
                issued[0] += 1

        cur = [0]
        first_x = []

        def next_slab():
            n = cur[0]
            cur[0] += 1
            prefetch(n + NSLOT - 1)
            return n % NSLOT

        S.dma(par[:, :], par_d[:, :], writes=[b_par])
        S.dma(idt[:, :], id_d[:, :], writes=[b_id])
        S.op("dve", lambda e: e.memset(ones_f[:, :], 1.0), writes=[b_onesf])
        S.op("dve", lambda e: e.memset(epsb[:, :], EPS), writes=[b_eps])
        S.op("dve", lambda e: e.tensor_copy(ones[:, :], ones_f[:, :]), reads=[b_onesf], writes=[b_ones])
        for j in range(4):
            S.op("dve", lambda e, j=j: e.memset(ubh[:, j, :].bitcast(F32), 0.0), writes=[b_ubh[j]])
            S.op("dve", lambda e, j=j: e.memset(cvh[:, j, :], 0.0), writes=[b_cvh[j]])
        def load_sample_hist():
            S.dma(ubs[:, :, :, :], ubs_d.rearrange("p (j s k) -> p j s k", j=4, s=NSEQ_S), writes=b_ubs)
            S.dma(cvs[:, :, :, :], cvs_d.rearrange("p (j s k) -> p j s k", j=4, s=NSEQ_S), writes=b_cvs)

        def sample_hist_conv(j, part):
            accs = [accS[:, a, j, :].rearrange("p (s t) -> p s t", t=TS) for a in range(2)]
            if part == 0:
                S.op("dve", lambda e: e.tensor_scalar(accs[0], ubs[:, j, :, 0:8], pc(P_CBW + j * KB + 0), pc(P_CBB + j), ALU.mult, ALU.add),
                     reads=[b_ubs[j], b_par], writes=[b_accS[0][j]])
                S.op("dve", lambda e: e.tensor_scalar(accs[1], ubs[:, j, :, 1:9], pc(P_CBW + j * KB + 1), None, ALU.mult),
                     reads=[b_ubs[j], b_par], writes=[b_accS[1][j]])
            ks = range(2, 16) if part == 0 else range(16, KB - 1)
            for k in ks:
                a = k % 2
                S.op("dve", lambda e, k=k, a=a: e.scalar_tensor_tensor(accs[a], ubs[:, j, :, k:k + 8], pc(P_CBW + j * KB + k), accs[a], ALU.mult, ALU.add),
                     reads=[b_ubs[j], b_par, b_accS[a][j]], writes=[b_accS[a][j]])
            if part == 1:
                S.op("dve", lambda e: e.tensor_tensor(accs[0], accs[0], accs[1], ALU.add),
                     reads=[b_accS[0][j], b_accS[1][j]], writes=[b_accS[0][j]])

        def blk_of(col, blocks):
            for bi_, (n0, n1) in enumerate(blocks):
                if n0 <= col < n1:
                    return bi_
            raise ValueError

        def xv(g):
            return X[:, g % 2], b_xx[g % 2]

        def stats_act(g, bi_, rt, b_rt):
            xt_, bx = xv(g)
            n0, n1 = GROUPS[g][3][bi_]
            N = n1 - n0
            pb = 6 + bi_
            for dc in range(8):
                s = dc % 2
                if s == 0 or bi_ > 0:
                    S.op("act", lambda e, dc=dc, s=s: e.activation(sq[:, s, 0:N], xt_[:, dc, n0:n1], AF.Square),
                         reads=[bx[dc][bi_]], writes=[b_sq[s]])
                else:
                    S.op("dve", lambda e, dc=dc, s=s: e.tensor_tensor(sq[:, s, 0:N], xt_[:, dc, n0:n1], xt_[:, dc, n0:n1], ALU.mult),
                         reads=[bx[dc][bi_]], writes=[b_sq[s]])
                S.mm_group_part(ps[:, pb, 0:N], ones[:, :], sq[:, s, 0:N], first=(dc == 0), last=(dc == 7),
                                reads=[b_ones, b_sq[s]], writes=[b_ps[pb]])
            stats_finish(pb, n0, n1, bi_, rt, b_rt)

        def preload_sqrt():
            S.op("act", lambda e: e.activation(dmy[:, 1:2], epsb[:, 0:1], AF.Sqrt), reads=[b_eps], writes=[b_dmy])

        def stats_sqrt(pb, n0, n1, bi_, rt, b_rt):
            N = n1 - n0
            S.op("act", lambda e: e.activation(rt[:, n0:n1], ps[:, pb, 0:N], AF.Sqrt, bias=epsb[:, 0:1], scale=1.0 / D),
                 reads=[b_ps[pb], b_eps], writes=[b_rt[bi_]])

        def stats_recip(n0, n1, bi_, rt, b_rt):
            S.op("dve", lambda e: e.reciprocal(rt[:, n0:n1], rt[:, n0:n1]),
                 reads=[b_rt[bi_]], writes=[b_rt[bi_]])

        def stats_finish(pb, n0, n1, bi_, rt, b_rt):
            stats_sqrt(pb, n0, n1, bi_, rt, b_rt)
            stats_recip(n0, n1, bi_, rt, b_rt)

        def stat_sq(g, dc):
            xt_, bx = xv(g)
            items = []
            for bi_, (n0, n1) in enumerate(GROUPS[g][3]):
                N = n1 - n0
                s = bi_ % 2
                S.op("act", lambda e, dc=dc, s=s, n0=n0, n1=n1, N=N: e.activation(sq[:, s, 0:N], xt_[:, dc, n0:n1], AF.Square),
                     reads=[bx[dc][bi_]], writes=[b_sq[s]])
                items.append((bi_, s, N))
            return items

        def stat_mm(items, first, last):
            for (bi_, s, N) in items:
                S.mm_group_part(ps[:, 6 + bi_, 0:N], ones[:, :], sq[:, s, 0:N], first=first, last=last,
                                reads=[b_ones, b_sq[s]], writes=[b_ps[6 + bi_]])

        def stat_fin(g, bi_, rt, b_rt):
            n0, n1 = GROUPS[g][3][bi_]
            stats_finish(6 + bi_, n0, n1, bi_, rt, b_rt)

        def apply_h(g, bi_, gcol, dcs=range(8)):
            xt_, bx = xv(g)
            n0, n1 = GROUPS[g][3][bi_]
            for dc in dcs:
                S.op("dve", lambda e, dc=dc: e.scalar_tensor_tensor(
                    hT[:, dc, n0:n1], xt_[:, dc, n0:n1], pc(gcol + dc), bcr[:, n0:n1], ALU.mult, ALU.mult),
                    reads=[bx[dc][bi_], b_par, b_bcr[bi_]], writes=[b_h[dc][bi_]])

        def apply_out(g, bi_, dcs=range(8)):
            xt_, bx = xv(g)
            p0, Tp, Tsm, blocks = GROUPS[g][0], GROUPS[g][1], GROUPS[g][2], GROUPS[g][3]
            n0, n1 = blocks[bi_]
            N = n1 - n0
            last = (g == len(GROUPS) - 1)
            ring = [(ost[:, 0, :], b_ost[0]), (ost[:, 1, :], b_ost[1])]
            if last:
                ring += [(tmp[:, i, :], b_tmp[i]) for i in range(3)]
            for dc in dcs:
                st, bst = ring[(bi_ * 8 + dc) % len(ring)]
                S.op("dve", lambda e, dc=dc, st=st: e.scalar_tensor_tensor(
                    st[:, 0:N], xt_[:, dc, n0:n1], pc(P_GFIN + dc), bcf[:, n0:n1], ALU.mult, ALU.mult),
                    reads=[bx[dc][bi_], b_par, b_bcf[bi_]], writes=[bst])
                npr = min(n1, Tp) - n0
                q = "pool" if (last and dc % 2) else "sp"
                if npr > 0:
                    final_toks.append(S.dma(ypT_d[dc * 128:(dc + 1) * 128, p0 + n0:p0 + n0 + npr], st[:, 0:npr], reads=[bst], eng=q))
                if n1 > Tp:
                    s0 = max(n0, Tp) - n0
                    final_toks.append(S.dma(ysT_d[dc * 128:(dc + 1) * 128, :], st[:, s0:s0 + 128], reads=[bst], eng=q))

        def load_x(g):
            p0, Tp, Tsm, blocks = GROUPS[g][0], GROUPS[g][1], GROUPS[g][2], GROUPS[g][3]
            xt_, bx = xv(g)
            xpv = xpT_d.rearrange("(dc p) t -> p dc t", p=128)
            for bi_, (n0, n1) in enumerate(blocks):
                m1 = min(n1, Tp)
                if m1 > n0:
                    for h in range(2):
                        S.dma(xt_[:, 4 * h:4 * h + 4, n0:m1], xpv[:, 4 * h:4 * h + 4, p0 + n0:p0 + m1],
                              writes=[bx[dc][bi_] for dc in range(4 * h, 4 * h + 4)])
                if n1 > Tp:
                    S.dma(xt_[:, :, Tp:Tp + Tsm], xsT_d.rearrange("(dc p) t -> p dc t", p=128),
                          writes=[bx[dc][bi_] for dc in range(8)])

        def mm_group_part(out_ap, l, r, first, last, reads, writes):
            waits = S._waits("pe", S._deps(reads, writes if first else []))
            S.cnt["pe"] += 1
            tok = ("pe", S.sem["pe"], S.cnt["pe"])

            def run(e, waits=waits, tok=tok):
                for sh, val in waits:
                    e.wait_ge(sh, val)
                e.matmul(out_ap, l, r, start=first, stop=last).then_inc(tok[1], 1)

            S.q["pe"].append(run)
            S._commit("pe", tok, reads, writes if last else [])
            return tok

        S.mm_group_part = mm_group_part

        def proj(slot, mloc, kcs, rhs_t, b_rhs, blocks, wk=128, mw=2, fine=False, only=None, banks=None):
            res = []
            for bi_, (n0, n1) in enumerate(blocks):
                if only is not None and bi_ != only:
                    continue
                pb = psum() if banks is None else banks[bi_]
                pairs = [(wsl[:, slot, (kc * mw + mloc) * 128:(kc * mw + mloc + 1) * 128], rhs_t[:, kc, n0:n1]) for kc in range(kcs)]
                if fine:
                    for kc, (l, r) in enumerate(pairs):
                        S.mm_group_part(ps[:, pb, 0:n1 - n0], l, r, first=(kc == 0), last=(kc == kcs - 1),
                                        reads=[b_w[slot], b_rhs[kc][bi_]], writes=[b_ps[pb]])
                else:
                    S.mm_group(ps[:, pb, 0:n1 - n0], pairs,
                               reads=[b_w[slot]] + [b_rhs[kc][bi_] for kc in range(kcs)], writes=[b_ps[pb]])
                res.append((bi_, n0, n1, pb))
            return res

        def do_group(g, p0, Tp, Tsm, blocks, cblocks, prev_final=None):
            Tg = Tp + Tsm
            last_group = (g == len(GROUPS) - 1)
            xg, bxg = xv(g)
            nblk = len(blocks)

            DHALF = [(0, 16), (16, KB)]

            def build_diag(j):
                for half in range(2):
                    S.dma(diag[:, half, :, :].rearrange("p k m -> p (k m)"), dgw_d[j * 2 + half], writes=[b_diag[half]], eng="pool")

            build_diag(0)
            NDVE = 6 if last_group else 7

            def conv_acc(j):
                if j % 2 == 0:
                    return bcm[:, 0:Tp], b_bcm
                return ost[:, :, :].rearrange("p a n -> p (a n)")[:, 0:Tp], b_ost

            def conv_b_dve(j):
                jb = j % 2
                acc, bacc = conv_acc(j)
                S.op("dve", lambda e: e.tensor_scalar(acc, ubp[:, jb, 0:Tp], pc(P_CBW + j * KB + 0), None, ALU.mult),
                     reads=[b_ubp[jb], b_par], writes=bacc)
                for k in range(1, NDVE):
                    S.op("dve", lambda e, k=k: e.scalar_tensor_tensor(acc, ubp[:, jb, k:k + Tp], pc(P_CBW + j * KB + k), acc, ALU.mult, ALU.add),
                         reads=[b_ubp[jb], b_par] + bacc, writes=bacc)

            def conv_b_prompt(j):
                jb = j % 2
                acc, bacc = conv_acc(j)
                pbs = [psum() for _ in cblocks]
                for half, (k0, k1) in enumerate(DHALF):
                    for k in range(max(k0, NDVE), k1):
                        for (c0, c1), pb in zip(cblocks, pbs):
                            S.mm_group_part(ps[:, pb, 0:c1 - c0], diag[:, half, k - k0, :], ubp[:, jb, k + c0:k + c1], first=(k == NDVE), last=(k == KB - 1),
                                            reads=[b_diag[half], b_ubp[jb]], writes=[b_ps[pb]])
                if j < 3:
                    build_diag(j + 1)
                for (c0, c1), pb in zip(cblocks, pbs):
                    S.op("dve", lambda e, pb=pb, c0=c0, c1=c1: e.scalar_tensor_tensor(cbT[:, j, c0:c1], ps[:, pb, 0:c1 - c0], pc(P_CBB + j), acc[:, c0:c1], ALU.add, ALU.add),
                         reads=[b_ps[pb], b_par] + bacc, writes=b_y[4 + j])
                if not last_group:
                    S.op("dve", lambda e, j=j, jb=jb: e.tensor_copy(ubh[:, j, :], ubp[:, jb, Tp:Tp + 30]),
                         reads=[b_ubp[jb]], writes=[b_ubh[j]])
                else:
                    final_toks.append(S.dma(nbp_o[:, j * 30:(j + 1) * 30], ubp[:, jb, Tp:Tp + 30].bitcast(F32), reads=[b_ubp[jb]]))

            def conv_b_sample(j):
                acc = accS[:, 0, j, :].rearrange("p (s t) -> p s t", t=TS)
                accr = cbT[:, j, Tp:Tp + 128].rearrange("p (s t) -> p s t", t=TS)
                for k in range(KB - TS, KB):
                    d = KB - 1 - k
                    lastk = (k == KB - 1)
                    o = accr if lastk else acc
                    S.op("dve", lambda e, k=k, d=d, o=o: e.scalar_tensor_tensor(o[:, :, d:TS], ubs[:, j, :, 30:30 + TS - d], pc(P_CBW + j * KB + k), acc[:, :, d:TS], ALU.mult, ALU.add),
                         reads=[b_ubs[j], b_par, b_accS[0][j]], writes=(b_y[4 + j] if lastk else [b_accS[0][j]]))

            pending = []
            for j in range(4):
                slot = next_slab()
                jb = j % 2
                S.op("dve", lambda e, j=j, jb=jb: e.tensor_copy(ubp[:, jb, 0:30], ubh[:, j, :]),
                     reads=[b_ubh[j]], writes=[b_ubp[jb]])
                if j == 0:
                    rv, rg = [], []
                    for ob in range(nblk):
                        rv += proj(slot, 0, 8, hT, b_h, blocks, fine=True, only=ob)
                        rg += proj(slot, 1, 8, hT, b_h, blocks, only=ob)
                else:
                    rv = proj(slot, 0, 8, hT, b_h, blocks)
                    rg = proj(slot, 1, 8, hT, b_h, blocks)
                for (bi_, n0, n1, pv), (_, _, _, pg) in zip(rv, rg):
                    N = n1 - n0
                    ts_ = bi_ % 3
                    S.op("act", lambda e, pg=pg, N=N, ts_=ts_: e.activation(tmp[:, ts_, 0:N], ps[:, pg, 0:N], AF.Sigmoid),
                         reads=[b_ps[pg]], writes=[b_tmp[ts_]])
                    npr = min(n1, Tp) - n0
                    if npr > 0:
                        S.op("dve", lambda e, pv=pv, ts_=ts_, jb=jb, n0=n0, npr=npr: e.tensor_tensor(ubp[:, jb, 30 + n0:30 + n0 + npr], ps[:, pv, 0:npr], tmp[:, ts_, 0:npr], ALU.mult),
                             reads=[b_ps[pv], b_tmp[ts_]], writes=[b_ubp[jb]])
                    if n1 > Tp:
                        s0 = max(n0, Tp) - n0
                        S.op("dve", lambda e, pv=pv, ts_=ts_, j=j, s0=s0: e.tensor_tensor(
                            ubs[:, j, :, 30:38], ps[:, pv, s0:s0 + 128].rearrange("p (s t) -> p s t", t=TS),
                            tmp[:, ts_, s0:s0 + 128].rearrange("p (s t) -> p s t", t=TS), ALU.mult),
                            reads=[b_ps[pv], b_tmp[ts_]], writes=[b_ubs[j]])
                conv_b_dve(j)
                if pending:
                    pending.pop(0)()
                pending.append(lambda j=j: (conv_b_prompt(j), conv_b_sample(j) if Tsm else None))
                if j == 3:
                    preload_sqrt()

            def ln_steps():
                p1, p2 = 6, 7

                def sqs(bi_, js):
                    n0, n1 = blocks[bi_]
                    N = n1 - n0
                    for j in js:
                        S.op("act", lambda e, j=j, s=j % 2: e.activation(sq[:, s, 0:N], cbT[:, j, n0:n1], AF.Square),
                             reads=[b_y[4 + j][bi_]], writes=[b_sq[j % 2]])

                def mms(bi_, js):
                    n0, n1 = blocks[bi_]
                    N = n1 - n0
                    for j in js:
                        S.mm_group_part(ps[:, p1, 0:N], ones[:, :], cbT[:, j, n0:n1], first=(j == 0), last=(j == 3),
                                        reads=[b_ones, b_y[4 + j][bi_]], writes=[b_ps[p1]])
                        S.mm_group_part(ps[:, p2, 0:N], ones[:, :], sq[:, j % 2, 0:N], first=(j == 0), last=(j == 3),
                                        reads=[b_ones, b_sq[j % 2]], writes=[b_ps[p2]])

                def fin(bi_):
                    n0, n1 = blocks[bi_]
                    N = n1 - n0
                    S.op("act", lambda e: e.activation(bcm[:, n0:n1], ps[:, p1, 0:N], AF.Copy, scale=1.0 / GW),
                         reads=[b_ps[p1]], writes=[b_bcm[bi_]])
                    S.op("act", lambda e: e.activation(tmp[:, 2, 0:N], bcm[:, n0:n1], AF.Square),
                         reads=[b_bcm[bi_]], writes=[b_tmp[2]])
                    S.op("dve", lambda e: e.scalar_tensor_tensor(tmp[:, 2, 0:N], ps[:, p2, 0:N], 1.0 / GW, tmp[:, 2, 0:N], ALU.mult, ALU.subtract),
                         reads=[b_ps[p2], b_tmp[2]], writes=[b_tmp[2]])
                    S.op("act", lambda e: e.activation(bcr[:, n0:n1], tmp[:, 2, 0:N], AF.Sqrt, bias=epsb[:, 0:1], scale=1.0),
                         reads=[b_tmp[2], b_eps], writes=[b_bcr[bi_]])
                    S.op("dve", lambda e: e.reciprocal(bcr[:, n0:n1], bcr[:, n0:n1]),
                         reads=[b_bcr[bi_]], writes=[b_bcr[bi_]])

                return [
                    lambda: sqs(0, (0, 1)),
                    lambda: (mms(0, (0, 1)), sqs(0, (2, 3))),
                    lambda: (mms(0, (2, 3)), sqs(1, (0, 1)), fin(0)),
                    lambda: (mms(1, (0, 1)), sqs(1, (2, 3))),
                    lambda: (mms(1, (2, 3)), fin(1)),
                ]

            def ln_part2(bi_):
                n0, n1 = blocks[bi_]
                N = n1 - n0
                for j in (0, 2, 1, 3):
                    if j < 2:
                        eng, scr, bscr = "dve", tmp[:, j % 2, 0:N], b_tmp[j % 2]
                    else:
                        eng, scr, bscr = "pool", ost[:, j % 2, 0:N], b_ost[j % 2]
                    S.op(eng, lambda e, j=j, scr=scr: e.tensor_tensor(scr, cbT[:, j, n0:n1], bcm[:, n0:n1], ALU.subtract),
                         reads=[b_y[4 + j][bi_], b_bcm[bi_]], writes=[bscr])
                    S.op(eng, lambda e, scr=scr: e.tensor_tensor(scr, scr, bcr[:, n0:n1], ALU.mult),
                         reads=[bscr, b_bcr[bi_]], writes=[bscr])
                    S.op("act", lambda e, j=j, scr=scr: e.activation(yfT[:, 4 + j, n0:n1], scr, AF.Silu, bias=pc(P_LNB + j), scale=pc(P_LNG + j)),
                         reads=[bscr, b_par], writes=[b_y[4 + j][bi_]])

            for j in range(4):
                slot = next_slab()
                jb = j % 2
                S.op("dve", lambda e, j=j, jb=jb: e.tensor_copy(cvp[:, jb, 0:2], cvh[:, j, :]),
                     reads=[b_cvh[j]], writes=[b_cvp[jb]])
                rc = proj(slot, 0, 8, hT, b_h, blocks)
                rv = proj(slot, 1, 8, hT, b_h, blocks)
                for (bi_, n0, n1, pcg), (_, _, _, pv) in zip(rc, rv):
                    N = n1 - n0
                    ts_ = bi_ % 2
                    S.op("act", lambda e, pcg=pcg, N=N, ts_=ts_: e.activation(tmp[:, ts_, 0:N], ps[:, pcg, 0:N], AF.Copy),
                         reads=[b_ps[pcg]], writes=[b_tmp[ts_]])
                    npr = min(n1, Tp) - n0
                    if npr > 0:
                        S.op("dve", lambda e, pv=pv, ts_=ts_, jb=jb, n0=n0, npr=npr: e.tensor_tensor(cvp[:, jb, 2 + n0:2 + n0 + npr], ps[:, pv, 0:npr], tmp[:, ts_, 0:npr], ALU.mult),
                             reads=[b_ps[pv], b_tmp[ts_]], writes=[b_cvp[jb]])
                    if n1 > Tp:
                        s0 = max(n0, Tp) - n0
                        S.op("dve", lambda e, pv=pv, ts_=ts_, j=j, s0=s0: e.tensor_tensor(
                            cvs[:, j, :, 2:10], ps[:, pv, s0:s0 + 128].rearrange("p (s t) -> p s t", t=TS),
                            tmp[:, ts_, s0:s0 + 128].rearrange("p (s t) -> p s t", t=TS), ALU.mult),
                            reads=[b_ps[pv], b_tmp[ts_]], writes=[b_cvs[j]])
                S.op("act", lambda e, j=j, jb=jb: e.activation(caT[:, j, 0:Tp], cvp[:, jb, 2:2 + Tp], AF.Identity, scale=pc(P_CAW + j * 3 + 2)),
                     reads=[b_cvp[jb], b_par], writes=b_y[j])
                for k in range(2):
                    S.op("dve", lambda e, j=j, jb=jb, k=k: e.scalar_tensor_tensor(caT[:, j, 0:Tp], cvp[:, jb, k:k + Tp], pc(P_CAW + j * 3 + k), caT[:, j, 0:Tp], ALU.mult, ALU.add),
                         reads=[b_cvp[jb], b_par] + b_y[j], writes=b_y[j])
                if Tsm:
                    accs = caT[:, j, Tp:Tp + 128].rearrange("p (s t) -> p s t", t=TS)
                    S.op("dve", lambda e, j=j, accs=accs: e.tensor_scalar(accs, cvs[:, j, :, 2:10], pc(P_CAW + j * 3 + 2), None, ALU.mult),
                         reads=[b_cvs[j], b_par], writes=b_y[j])
                    for k in range(2):
                        S.op("dve", lambda e, j=j, k=k, accs=accs: e.scalar_tensor_tensor(accs, cvs[:, j, :, k:k + 8], pc(P_CAW + j * 3 + k), accs, ALU.mult, ALU.add),
                             reads=[b_cvs[j], b_par] + b_y[j], writes=b_y[j])
                if not last_group:
                    S.op("dve", lambda e, j=j, jb=jb: e.tensor_copy(cvh[:, j, :], cvp[:, jb, Tp:Tp + 2]),
                         reads=[b_cvp[jb]], writes=[b_cvh[j]])
                else:
                    final_toks.append(S.dma(nap_o[:, j * 2:(j + 1) * 2], cvp[:, jb, Tp:Tp + 2], reads=[b_cvp[jb]]))
                if pending:
                    pending.pop(0)()
                if j == 0:
                    lnq = ln_steps()
                    lnq.pop(0)()
                elif j == 1:
                    lnq.pop(0)()
                    lnq.pop(0)()
                elif j == 2:
                    lnq.pop(0)()
                    lnq.pop(0)()
                    ln_part2(0)
                else:
                    ln_part2(1)
            while pending:
                pending.pop(0)()
            while lnq:
                lnq.pop(0)()
            preload_sqrt()
            for jj in range(2):
                slot = next_slab()
                for m in range(2):
                    j = 2 * jj + m
                    bk = [6, 7] if (jj == 0 and m == 1) else None
                    for (bi_, n0, n1, pb) in proj(slot, m, 8, hT, b_h, blocks, banks=bk):
                        N = n1 - n0
                        S.op("dve", lambda e, pb=pb, j=j, n0=n0, n1=n1, N=N: e.tensor_tensor(yfT[:, j, n0:n1], ps[:, pb, 0:N], caT[:, j, n0:n1], ALU.mult),
                             reads=[b_ps[pb], b_y[j][bi_]], writes=[b_y[j][bi_]])
            if last_group:
                final_toks.append(S.dma(ubs_o.rearrange("p (j s k) -> p j s k", j=4, s=NSEQ_S), ubs[:, :, :, :], reads=b_ubs))
                final_toks.append(S.dma(cvs_o.rearrange("p (j s k) -> p j s k", j=4, s=NSEQ_S), cvs[:, :, :, :], reads=b_cvs))

            pend = None
            for sl in range(4):
                slot = next_slab()
                for m in range(2):
                    dc = 2 * sl + m
                    for (bi_, n0, n1, pb) in proj(slot, m, 8, yfT, b_y, blocks):
                        N = n1 - n0
                        S.op("dve", lambda e, pb=pb, dc=dc, n0=n0, n1=n1, N=N: e.tensor_tensor(xg[:, dc, n0:n1], xg[:, dc, n0:n1], ps[:, pb, 0:N], ALU.add),
                             reads=[b_ps[pb], bxg[dc][bi_]], writes=[bxg[dc][bi_]])
                    if pend is not None:
                        stat_mm(pend[1], first=(pend[0] == 0), last=False)
                    pend = (dc, stat_sq(g, dc))
            stat_mm(pend[1], first=False, last=True)
            for bi_ in range(nblk):
                stat_fin(g, bi_, bcr, b_bcr)
                apply_h(g, bi_, P_GFFN)
            for hf in range(2):
                hooks = []
                if hf == 0:
                    hooks = list(prev_final or [])
                    if g == 0:
                        hooks.append(load_sample_hist)
                    if not last_group:
                        hooks.append(lambda: load_x(g + 1))
                n1pend = None
                if hf == 1 and g == 0:
                    hooks = [lambda j=j, part=part: sample_hist_conv(j, part) for j in range(4) for part in range(2)]
                for c in range(11):
                    if c >= 1 and hooks:
                        hooks.pop(0)()
                    if hf == 1 and not last_group and c < 9:
                        if n1pend is not None:
                            stat_mm(n1pend[1], first=(n1pend[0] == 0), last=(n1pend[0] == 7))
                        n1pend = (c, stat_sq(g + 1, c)) if c < 8 else None
                    slot = next_slab()
                    if hf == 0 and c == 0:
                        rg, ru = [], []
                        for ob in range(nblk):
                            rg += proj(slot, 0, 8, hT, b_h, blocks, fine=True, only=ob)
                            ru += proj(slot, 1, 8, hT, b_h, blocks, only=ob)
                    else:
                        rg = proj(slot, 0, 8, hT, b_h, blocks)
                        ru = proj(slot, 1, 8, hT, b_h, blocks)
                    for (bi_, n0, n1, pg), (_, _, _, pu) in zip(rg, ru):
                        N = n1 - n0
                        ts_ = (c * 2 + bi_) % 3
                        S.op("act", lambda e, pg=pg, N=N, ts_=ts_: e.activation(tmp[:, ts_, 0:N], ps[:, pg, 0:N], AF.Silu),
                             reads=[b_ps[pg]], writes=[b_tmp[ts_]])
                        S.op("dve", lambda e, pu=pu, ts_=ts_, c=c, n0=n0, n1=n1, N=N: e.tensor_tensor(yfT[:, c, n0:n1], ps[:, pu, 0:N], tmp[:, ts_, 0:N], ALU.mult),
                             reads=[b_ps[pu], b_tmp[ts_]], writes=[b_y[c][bi_]])
                n1q = []
                if hf == 1 and not last_group:
                    nb = GROUPS[g + 1][3]
                    n1q = [lambda: (stats_sqrt(6, nb[0][0], nb[0][1], 0, bcr, b_bcr), stats_sqrt(7, nb[1][0], nb[1][1], 1, bcr, b_bcr),
                                    stats_recip(nb[0][0], nb[0][1], 0, bcr, b_bcr))]
                    n1q += [lambda d0=d0: apply_h(g + 1, 0, P_GMIX, range(d0, d0 + 2)) for d0 in range(0, 8, 2)]
                    n1q += [lambda: (stats_recip(nb[1][0], nb[1][1], 1, bcr, b_bcr), apply_h(g + 1, 1, P_GMIX, range(0, 3))),
                            lambda: apply_h(g + 1, 1, P_GMIX, range(3, 6)), lambda: apply_h(g + 1, 1, P_GMIX, range(6, 8))]
                pend = None
                if hf == 1 and last_group:
                    preload_sqrt()
                for dc in range(8):
                    if n1q:
                        n1q.pop(0)()
                    slot = next_slab()
                    for (bi_, n0, n1, pb) in proj(slot, 0, 11, yfT, b_y, blocks, mw=1):
                        N = n1 - n0
                        S.op("dve", lambda e, pb=pb, dc=dc, n0=n0, n1=n1, N=N: e.tensor_tensor(xg[:, dc, n0:n1], xg[:, dc, n0:n1], ps[:, pb, 0:N], ALU.add),
                             reads=[b_ps[pb], bxg[dc][bi_]], writes=[bxg[dc][bi_]])
                    if hf == 1:
                        if pend is not None:
                            stat_mm(pend[1], first=(pend[0] == 0), last=False)
                        pend = (dc, stat_sq(g, dc))
                if hf == 1:
                    stat_mm(pend[1], first=False, last=True)
                    for bi_ in range(nblk):
                        stat_fin(g, bi_, bcf, b_bcf)
            return [lambda bi_=bi_, d0=d0: apply_out(g, bi_, range(d0, d0 + 2)) for bi_ in range(nblk) for d0 in range(0, 8, 2)]

        load_x(0)
        first_x.extend(b_xx[0][dc][0] for dc in range(8))
        prefetch(NSLOT - 2)
        for bi_ in range(len(GROUPS[0][3])):
            stats_act(0, bi_, bcr, b_bcr)
            apply_h(0, bi_, P_GMIX)
        deferred = None
        for g, grp in enumerate(GROUPS):
            deferred = do_group(g, *grp, prev_final=deferred)
        for st in deferred:
            st()

        S.finish(final_toks)
    return nc


def _prep_weights(w_in, w_out, w_gate, w_up, w_down):
    def slab(cols_src, kc):
        return np.ascontiguousarray(cols_src.reshape(kc, 128, -1).transpose(1, 0, 2)).reshape(128, -1)

    w_in = w_in[0]
    cc = lambda a, i: a[:, i * 128:(i + 1) * 128]
    B, C, V, GV, GG = 0, 4, 8, 12, 16
    w8 = np.empty((36, 128, 2048), np.float32)
    n = 0
    for j in range(4):
        w8[n] = slab(np.concatenate([cc(w_in, GV + j), cc(w_in, GG + j)], 1), 8); n += 1
    for j in range(4):
        w8[n] = slab(np.concatenate([cc(w_in, C + j), cc(w_in, V + j)], 1), 8); n += 1
    for jj in range(2):
        w8[n] = slab(np.concatenate([cc(w_in, B + 2 * jj), cc(w_in, B + 2 * jj + 1)], 1), 8); n += 1
    wo = w_out[0]
    for sl in range(4):
        w8[n] = slab(wo[:, sl * 256:(sl + 1) * 256], 8); n += 1
    wg, wu = w_gate[0], w_up[0]
    for c in range(22):
        w8[n] = slab(np.concatenate([cc(wg, c), cc(wu, c)], 1), 8); n += 1
    assert n == 36
    wdn = w_down[0]
    wd = np.empty((16, 128, 1408), np.float32)
    for hf in range(2):
        for dc in range(8):
            wd[hf * 8 + dc] = slab(wdn[hf * 1408:(hf + 1) * 1408, dc * 128:(dc + 1) * 128], 11)
    return w8, wd


def _prep_params(g_mix, g_ffn, g_final, conv_a_w, conv_b_w, conv_b_bias, ln_b_g, ln_b_b):
    p = np.empty((128, NPARAM), np.float32)
    p[:, P_GMIX:P_GMIX + 8] = g_mix[0].reshape(8, 128).T
    p[:, P_GFFN:P_GFFN + 8] = g_ffn[0].reshape(8, 128).T
    p[:, P_GFIN:P_GFIN + 8] = g_final.reshape(8, 128).T
    p[:, P_CAW:P_CAW + 12] = conv_a_w[0].reshape(3, 4, 128).transpose(2, 1, 0).reshape(128, 12)
    p[:, P_CBW:P_CBW + 124] = conv_b_w[0].reshape(31, 4, 128).transpose(2, 1, 0).reshape(128, 124)
    p[:, P_CBB:P_CBB + 4] = conv_b_bias[0].reshape(4, 128).T
    p[:, P_LNG:P_LNG + 4] = ln_b_g[0].reshape(4, 128).T
    p[:, P_LNB:P_LNB + 4] = ln_b_b[0].reshape(4, 128).T
    return p


_NC_CACHE = {}


def _make_in_maps(x_prompt, x_sample, state_conv_a, state_conv_b, g_mix, w_in, conv_a_w, conv_b_w,
                  conv_b_bias, ln_b_g, ln_b_b, w_out, g_ffn, w_gate, w_up, w_down, g_final, cores=None):
    f = lambda a: np.ascontiguousarray(np.asarray(a, dtype=np.float32))
    x_prompt, x_sample, state_conv_a, state_conv_b = f(x_prompt), f(x_sample), f(state_conv_a), f(state_conv_b)
    w8, wd = _prep_weights(f(w_in), f(w_out), f(w_gate), f(w_up), f(w_down))
    params = _prep_params(f(g_mix), f(g_ffn), f(g_final), f(conv_a_w), f(conv_b_w), f(conv_b_bias), f(ln_b_g), f(ln_b_b))
    ident = np.eye(128, dtype=np.float32)
    cbw = f(conv_b_w)[0].reshape(KB, 4, 128)
    dgw = np.zeros((4, 2, 128, 16, 128), np.float32)
    ar = np.arange(128)
    for j in range(4):
        for k in range(KB):
            dgw[j, k // 16, ar, k % 16, ar] = cbw[k, j]
    dgw = dgw.reshape(8, 128, 2048)
    in_maps = []
    for c in (range(NCORES) if cores is None else cores):
        sl = slice(c * NSEQ_S, (c + 1) * NSEQ_S)
        hb = state_conv_b[0, sl].reshape(NSEQ_S, KB - 1, 4, 128).transpose(3, 2, 0, 1)
        ubs_in = np.zeros((128, 4, NSEQ_S, KB - 1 + TS), np.float32)
        ubs_in[..., :KB - 1] = hb
        ha = state_conv_a[0, sl].reshape(NSEQ_S, KA - 1, 4, 128).transpose(3, 2, 0, 1)
        cvs_in = np.zeros((128, 4, NSEQ_S, KA - 1 + TS), np.float32)
        cvs_in[..., :KA - 1] = ha
        in_maps.append({
            "xpT": np.ascontiguousarray(x_prompt[c].T),
            "xsT": np.ascontiguousarray(x_sample[sl].reshape(NSEQ_S * TS, D).T),
            "ubs_in": ubs_in.reshape(128, -1), "cvs_in": cvs_in.reshape(128, -1),
            "w8": w8, "wd": wd, "params": params, "ident": ident, "dgw": dgw,
        })
    return in_maps


def _assemble_core(r):
    yp = np.ascontiguousarray(r["ypT"].T)
    ys = np.ascontiguousarray(r["ysT"].T).reshape(NSEQ_S, TS, D)
    nbs = r["ubs_out"].reshape(128, 4, NSEQ_S, KB - 1 + TS)[:, :, :, TS:].transpose(2, 3, 1, 0).reshape(NSEQ_S, KB - 1, GW)
    nas = r["cvs_out"].reshape(128, 4, NSEQ_S, KA - 1 + TS)[:, :, :, TS:].transpose(2, 3, 1, 0).reshape(NSEQ_S, KA - 1, GW)
    nbp = r["nbp_out"].reshape(128, 4, KB - 1).transpose(2, 1, 0).reshape(KB - 1, GW)
    nap = r["nap_out"].reshape(128, 4, KA - 1).transpose(2, 1, 0).reshape(KA - 1, GW)
    return yp, ys, nap, nbp, nas, nbs


def kernel(**inputs):
    in_maps = _make_in_maps(**inputs)
    if "nc" not in _NC_CACHE:
        _NC_CACHE["nc"] = build_nc()
    nc = _NC_CACHE["nc"]
    res = run_bass_kernel_spmd(nc, in_maps, core_ids=list(range(NCORES)))
    outs = [_assemble_core(r) for r in res.results]
    f32 = lambda a: np.ascontiguousarray(a, dtype=np.float32)
    y_prompt = f32(np.stack([o[0] for o in outs], 0))
    y_sample = f32(np.concatenate([o[1] for o in outs], 0))
    nap = f32(np.stack([o[2] for o in outs], 0)[None])
    nbp = f32(np.stack([o[3] for o in outs], 0)[None])
    nas = f32(np.concatenate([o[4] for o in outs], 0)[None])
    nbs = f32(np.concatenate([o[5] for o in outs], 0)[None])
    return (y_prompt, y_sample, nap, nbp, nas, nbs)
```
